# Optimizing a Trainium2 kernel written in Bass

```python
import jax, jax.numpy as jnp
from jax import lax
import numpy as np

D_MODEL = 1024
BATCH = 4
SEQ = 8192
DEPTH = 1

D_CONV = D_MODEL
CONV_GROUPS = 8
CONV_A_WIDTH = 3
D_RNN = D_MODEL
RNN_HEADS = 4
RNN_BLOCK = D_RNN // RNN_HEADS
CONV_B_WIDTH = 4
LRU_C = 8.0
D_FF = ((8 * D_MODEL + 3 * 256 - 1) // (3 * 256)) * 256
N_MOD = 6
EPS = 1e-6
IN_WIDTHS = (D_CONV, D_CONV, D_CONV, D_RNN, D_RNN, D_MODEL, D_MODEL)
IN_TOTAL = sum(IN_WIDTHS)
IN_SPLITS = tuple(int(v) for v in np.cumsum(IN_WIDTHS)[:-1])

kernel_name = "hybrid_conv_rglru_gated_merge_adaln"


def rmsnorm(x, g):
    xf = x.astype(jnp.float32)
    y = xf * lax.rsqrt(jnp.mean(xf * xf, axis=-1, keepdims=True) + EPS) * g.astype(jnp.float32)
    return y.astype(x.dtype)


def modulate(h, shift, scale):
    return h * (1.0 + scale[:, None, :]) + shift[:, None, :]


def causal_depthwise_conv(u, w):
    k, ch = w.shape
    return lax.conv_general_dilated(
        u, w[:, None, :].astype(u.dtype), window_strides=(1,), padding=[(k - 1, 0)],
        dimension_numbers=("NWC", "WIO", "NWC"), feature_group_count=ch)


def block_diag_linear(u, w, b):
    bs, s, d = u.shape
    uh = u.reshape(bs, s, RNN_HEADS, RNN_BLOCK)
    return jnp.einsum("bshi,hij->bshj", uh, w).reshape(bs, s, d) + b


def rg_lru(u, w_a, b_a, w_x, b_x, lam):
    r = jax.nn.sigmoid(block_diag_linear(u, w_a, b_a)).astype(jnp.float32)
    i = jax.nn.sigmoid(block_diag_linear(u, w_x, b_x))
    log_a = LRU_C * r * jax.nn.log_sigmoid(lam.astype(jnp.float32))
    a = jnp.exp(log_a)
    mult = jnp.sqrt(jnp.maximum(-jnp.expm1(2.0 * log_a), 0.0))
    mult = mult.at[:, 0].set(1.0)
    bx = mult * (i * u).astype(jnp.float32)

    def combine(left, right):
        a1, b1 = left
        a2, b2 = right
        return a1 * a2, a2 * b1 + b2

    _, h = lax.associative_scan(combine, (a, bx), axis=1)
    return h.astype(u.dtype)


def setup_inputs(seed: int = 0) -> dict:
    key = jax.random.key(seed)
    ks = jax.random.split(key, 20)
    f32 = jnp.float32
    nrm = lambda k, shape, s: jax.random.normal(k, shape, f32) * s
    a8 = jax.random.uniform(ks[11], (DEPTH, D_RNN), f32, 0.9, 0.999)
    base = a8 ** (1.0 / LRU_C)
    lru_lambda = jnp.log(base) - jnp.log1p(-base)
    return {
        "x": nrm(ks[0], (BATCH, SEQ, D_MODEL), 1.0),
        "c": nrm(ks[1], (BATCH, D_MODEL), 1.0),
        "w_ada": nrm(ks[2], (DEPTH, D_MODEL, N_MOD * D_MODEL), 0.5 * D_MODEL ** -0.5),
        "b_ada": nrm(ks[3], (DEPTH, N_MOD * D_MODEL), 0.01),
        "g_norm_mix": 1.0 + nrm(ks[4], (DEPTH, D_MODEL), 0.05),
        "w_in": nrm(ks[5], (DEPTH, D_MODEL, IN_TOTAL), D_MODEL ** -0.5),
        "conv_a_w": nrm(ks[6], (DEPTH, CONV_A_WIDTH, D_CONV), CONV_A_WIDTH ** -0.5),
        "conv_b_w": nrm(ks[7], (DEPTH, CONV_B_WIDTH, D_RNN), CONV_B_WIDTH ** -0.5),
        "conv_b_bias": nrm(ks[8], (DEPTH, D_RNN), 0.01),
        "w_rg_a": nrm(ks[9], (DEPTH, RNN_HEADS, RNN_BLOCK, RNN_BLOCK), RNN_BLOCK ** -0.5),
        "b_rg_a": nrm(ks[10], (DEPTH, D_RNN), 0.01),
        "w_rg_x": nrm(ks[12], (DEPTH, RNN_HEADS, RNN_BLOCK, RNN_BLOCK), RNN_BLOCK ** -0.5),
        "b_rg_x": nrm(ks[13], (DEPTH, D_RNN), 0.01),
        "lru_lambda": lru_lambda,
        "w_out": nrm(ks[14], (DEPTH, D_MODEL, D_MODEL), D_MODEL ** -0.5),
        "g_norm_ffn": 1.0 + nrm(ks[15], (DEPTH, D_MODEL), 0.05),
        "w_gate_up": nrm(ks[16], (DEPTH, D_MODEL, 2 * D_FF), D_MODEL ** -0.5),
        "w_down": nrm(ks[17], (DEPTH, D_FF, D_MODEL), D_FF ** -0.5),
        "g_norm_final": 1.0 + nrm(ks[18], (D_MODEL,), 0.05),
    }


def reference(x, c, w_ada, b_ada, g_norm_mix, w_in, conv_a_w, conv_b_w, conv_b_bias,
              w_rg_a, b_rg_a, w_rg_x, b_rg_x, lru_lambda, w_out, g_norm_ffn,
              w_gate_up, w_down, g_norm_final):
    c_act = jax.nn.silu(c)
    for l in range(DEPTH):
        mod = c_act @ w_ada[l] + b_ada[l]
        sh1, sc1, gt1, sh2, sc2, gt2 = jnp.split(mod, N_MOD, axis=-1)

        h = modulate(rmsnorm(x, g_norm_mix[l]), sh1, sc1)
        proj = h @ w_in[l]
        cb, cc, cx, rx, rg, ga, gb = jnp.split(proj, IN_SPLITS, axis=-1)
        y_a = cb * causal_depthwise_conv(cc * cx, conv_a_w[l])
        u = causal_depthwise_conv(rx, conv_b_w[l]) + conv_b_bias[l]
        y_b = rg_lru(u, w_rg_a[l], b_rg_a[l], w_rg_x[l], b_rg_x[l], lru_lambda[l]) * jax.nn.gelu(rg)
        merged = jax.nn.sigmoid(ga) * y_a + jax.nn.sigmoid(gb) * y_b
        x = x + gt1[:, None, :] * (merged @ w_out[l])

        h = modulate(rmsnorm(x, g_norm_ffn[l]), sh2, sc2)
        g_ff, u_ff = jnp.split(h @ w_gate_up[l], 2, axis=-1)
        x = x + gt2[:, None, :] * ((jax.nn.silu(g_ff) * u_ff) @ w_down[l])
    return rmsnorm(x, g_norm_final)
```

```python
import numpy as np
from contextlib import ExitStack
import concourse.bass as bass
import concourse.mybir as mybir
from concourse.bass_utils import run_bass_kernel_spmd

F32 = mybir.dt.float32
BF16 = mybir.dt.bfloat16
AF = mybir.ActivationFunctionType
ALU = mybir.AluOpType

D = 1024
KC = 8
T = 512
NB = 4
NT = 8
FF = 2816
NF = 22
NCORES = 8
TOK = 4096
EPS = 1e-6
NSLOT = 4
NV = 21


class Buf:
    __slots__ = ("w", "r")

    def __init__(self):
        self.w = None
        self.r = {}


class Sched:
    def __init__(self):
        self.E = {}

    def add_engine(self, name, eng, sem):
        self.E[name] = dict(name=name, eng=eng, sem=sem, count=0, prog=[], waited={})

    def _wait(self, E, tok):
        sem, val, owner = tok
        if owner == "pe" and E["name"] == "pe":
            return
        key = id(sem)
        if E["waited"].get(key, 0) >= val:
            return
        E["waited"][key] = val
        eng = E["eng"]
        E["prog"].append(lambda: eng.wait_ge(sem, val))

    def _deps(self, E, reads, writes):
        for b in reads:
            if b.w is not None:
                self._wait(E, b.w)
        for b in writes:
            if b.w is not None:
                self._wait(E, b.w)
            for t in b.r.values():
                self._wait(E, t)

    @staticmethod
    def _upd(tok, reads, writes):
        for b in writes:
            b.w = tok
            b.r = {}
        for b in reads:
            if b.w is tok:
                continue
            b.r[id(tok[0])] = tok

    def op(self, ename, fn, reads=(), writes=(), inc=True):
        E = self.E[ename]
        self._deps(E, reads, writes)
        seq = E["count"] + 1
        sem = E["sem"]
        if inc:
            E["count"] = seq
            E["prog"].append(lambda: fn().then_inc(sem, 1))
        else:
            E["prog"].append(fn)
        self._upd((sem, seq, ename), reads, writes)

    def dma(self, qname, fn, semc, reads=(), writes=(), nodeps=False):
        E = self.E[qname]
        if not nodeps:
            self._deps(E, reads, writes)
        semc[1] += 16
        sem = semc[0]
        E["prog"].append(lambda: fn().then_inc(sem, 16))
        self._upd((sem, semc[1], "dma"), reads, writes)


class _Stop(Exception):
    pass


def build_nc(limit=None, dbg=None):
    nc = bass.Bass("TRN2", target_bir_lowering=False)
    _ck = [0]

    def ckpt(name):
        _ck[0] += 1
        if limit is not None and _ck[0] == limit:
            print("STOP at ckpt", _ck[0], name)
            _ck.append("stopped")

    def stopped():
        return len(_ck) > 1

    def din(name, shape, dt=F32):
        return nc.dram_tensor(name, shape, dt, kind="ExternalInput").ap()

    x_main = din("x_main", [TOK, D])
    x_prev = din("x_prev", [TOK, D])
    vecs = din("vecs", [NV * 8, 128])
    flags = din("flags", [128, 2])
    w_ada = din("w_ada", [D, 6 * D])
    w_in = din("w_in", [D, 7 * D])
    w_rg_a = din("w_rg_a", [4, 256, 256])
    w_rg_x = din("w_rg_x", [4, 256, 256])
    w_out = din("w_out", [D, D])
    w_gu = din("w_gu", [D, 2 * FF])
    w_down = din("w_down", [FF, D])
    out = nc.dram_tensor("out", [TOK, D], F32, kind="ExternalOutput").ap()

    def dscr(name, shape):
        return nc.dram_tensor(name, shape, BF16, kind="Internal").ap()

    win_bf = dscr("win_bf", [D, 7 * D])
    wgu_bf = dscr("wgu_bf", [D, 2 * FF])
    wdown_bf = dscr("wdown_bf", [FF, D])
    wout_bf = dscr("wout_bf", [D, D])
    wrga_bf = dscr("wrga_bf", [4, 256, 256])
    wrgx_bf = dscr("wrgx_bf", [4, 256, 256])

    with ExitStack() as es:
        def sb(name, shape, dt=F32):
            return es.enter_context(nc.sbuf_tensor(name, shape, dt))

        def ps(name, shape, dt=F32):
            return es.enter_context(nc.psum_tensor(name, shape, dt))

        def semc(name):
            return [es.enter_context(nc.semaphore(name)), 0]

        S = Sched()
        ENG = dict(pe=nc.tensor, act=nc.scalar, dve=nc.vector, pool=nc.gpsimd, sp=nc.sync)
        for nm, e in ENG.items():
            S.add_engine(nm, e, es.enter_context(nc.semaphore("s_" + nm)))

        def op(e, meth, reads, writes, *args, inc=True, **kw):
            if stopped():
                return
            eng = ENG[e]
            S.op(e, lambda: getattr(eng, meth)(*args, **kw), reads, writes, inc)

        def dma(q, out_ap, in_ap, sc, reads, writes, nodeps=False):
            if stopped():
                return
            eng = ENG[q]
            S.dma(q, lambda: eng.dma_start(out=out_ap, in_=in_ap), sc, reads, writes, nodeps)

        xt = [sb(f"xt{i}", [128, NB, D]) for i in range(2)]
        b_xt = [[Buf() for _ in range(NB)] for _ in range(2)]
        xn = sb("xn", [128, NB, D]); b_xn = [Buf() for _ in range(NB)]
        junk = sb("junk", [128, D], BF16); b_junk = Buf()
        hT = sb("hT", [128, KC, T], BF16); b_hT = [Buf() for _ in range(KC)]
        ring = sb("ring", [128, NSLOT, 4096], BF16); b_ring = [Buf() for _ in range(NSLOT)]
        ring_sem = [semc(f"rs{i}") for i in range(NSLOT)]
        ring_sem_sw = [semc(f"rw{i}") for i in range(NSLOT)]
        wga = sb("wga", [128, 4, 2, 256], BF16)
        wgx = sb("wgx", [128, 4, 2, 256], BF16)
        b_wg = Buf()
        vrow0 = sb("vrow0", [128, 128]); vrow1 = sb("vrow1", [NV * 8 - 128, 128]); b_vrow = Buf()
        vfm = sb("vfm", [128, NV * 8]); b_vfm = Buf()
        ident = sb("ident", [128, 128]); identb = sb("identb", [128, 128], BF16); b_id = Buf()
        ones = sb("ones", [128, 128]); b_ones = Buf()
        cst = sb("cst", [128, 16 * 8]); b_cst = Buf()
        cactb = sb("cactb", [128, 8], BF16)
        modv = sb("modv", [128, 48]); b_mod = Buf()
        flg = sb("flg", [128, 2]); b_flg = Buf()
        bc = sb("bc", [128, 3, D]); b_bc = Buf()
        diag = sb("diag", [128, 2, 128]); b_diag = [Buf(), Buf()]
        stat = sb("stat", [128, 3, 3, NB]); b_stat = [[[Buf() for _ in range(3)] for _ in range(NB)] for _ in range(3)]
        cm05 = sb("cm05", [128, NB]); b_cm05 = Buf()
        hstate = sb("hstate", [128, KC]); b_hs = Buf()
        halo_rx = sb("halo_rx", [128, KC, 3]); b_hrx = Buf()
        halo_p = sb("halo_p", [128, KC, 2]); b_hp = Buf()
        GC = 2
        rxs_s = [sb(f"rxs{i}", [128, GC, T + 3]) for i in range(2)]; b_rxs_s = [[Buf() for _ in range(GC)] for _ in range(2)]
        u_s = [sb(f"u{i}", [128, GC, T]) for i in range(2)]; b_u_s = [[Buf() for _ in range(GC)] for _ in range(2)]
        ubf_s = [sb(f"ubf{i}", [128, GC, T], BF16) for i in range(2)]; b_ubf_s = [[Buf() for _ in range(GC)] for _ in range(2)]
        Aa = sb("Aa", [128, GC, T]); b_A = [Buf() for _ in range(GC)]
        Mm = sb("Mm", [128, GC, T]); b_M = [Buf() for _ in range(GC)]
        TI = sb("TI", [128, GC, T]); b_TI = [Buf() for _ in range(GC)]
        trt = sb("trt", [128, GC, T]); b_trt = [Buf() for _ in range(GC)]
        hb = sb("hb", [128, GC, T]); b_hb = [Buf() for _ in range(GC)]
        gl = sb("gl", [128, GC, T], BF16); b_gl = [Buf() for _ in range(GC)]
        tb = sb("tb", [128, GC, T]); b_tb = [Buf() for _ in range(GC)]
        ta = sb("ta", [128, GC, T]); b_ta = [Buf() for _ in range(GC)]
        cxs = sb("cxs", [128, GC, T]); b_cxs = [Buf() for _ in range(GC)]
        pp = sb("pp", [128, GC, T + 2]); b_pp = [Buf() for _ in range(GC)]
        ya = sb("ya", [128, GC, T]); b_ya = [Buf() for _ in range(GC)]
        actT = sb("actT", [128, NF, T], BF16); b_actT = [Buf() for _ in range(NF)]
        mT = actT; b_mT = b_actT
        sg = sb("sg", [128, 2, T]); b_sg = [Buf(), Buf()]
        tmpg = sb("tmpg", [128, 2, T]); b_tmpg = [Buf(), Buf()]
        NPM = 8
        pm = [ps(f"pm{i}", [128, T]) for i in range(NPM)]; b_pm = [Buf() for _ in range(NPM)]
        st = dict(bank=0, slot=0, tp=0)

        def next_bank():
            st["bank"] = (st["bank"] + 1) % NPM
            return st["bank"]

        C_S1, C_SH1, C_S2, C_SH2, C_GT1H, C_GT2, C_HC, C_CC, C_HBA, C_HBX, C_TMP, C_TMP2 = [i * 8 for i in range(12)]
        V_C, V_BADA, V_GMIX, V_CAW, V_CBW, V_CBB, V_BRA, V_BRX, V_LAM, V_GFFN, V_GFIN = 0, 8, 56, 64, 88, 120, 128, 136, 144, 152, 160

        def vcol(base, j):
            return vfm[:, base + j:base + j + 1]

        def ccol(base, j):
            return cst[:, base + j:base + j + 1]

        b_s_g = Buf(); s_g = semc("cg")
        dma("pool", wrga_bf, w_rg_a, s_g, [], [b_s_g])
        dma("pool", wrgx_bf, w_rg_x, s_g, [], [b_s_g])
        s_small = semc("csm")
        vv = vecs
        dma("sp", vrow0[:], vv[0:128, :], s_small, [], [b_vrow])
        dma("sp", vrow1[:], vv[128:NV * 8, :], s_small, [], [b_vrow])
        s_flg = semc("cflg")
        dma("sp", flg[:], flags, s_flg, [], [b_flg])
        ada_slots = []

        def ring_view(slot, pat, **kw):
            return ring[:, slot, :].rearrange(pat, **kw)

        def load_piece(q, view_pat, view_kw, src, reads, nelem=4096):
            slot = st["slot"]
            st["slot"] = (slot + 1) % NSLOT
            dst = ring[:, slot, 0:nelem].rearrange(view_pat, **view_kw)
            dma(q, dst, src, (ring_sem_sw if q == "pool" else ring_sem)[slot], reads, [b_ring[slot]])
            return slot, dst

        op("pool", "memset", [], [b_id], ident[:], 0.0)
        op("pool", "affine_select", [b_id], [b_id], out=ident[:], in_=ident[:], compare_op=ALU.not_equal,
           fill=1.0, base=0, pattern=[[-1, 128]], channel_multiplier=1)
        op("dve", "tensor_copy", [b_id], [b_id], identb[:], ident[:])
        op("pool", "memset", [], [b_ones], ones[:], 1.0)
        op("pool", "memset", [], [b_cm05], cm05[:], -0.5)
        op("pool", "memset", [], [b_hs], hstate[:], 0.0)
        op("pool", "memset", [], [b_hrx], halo_rx[:], 0.0)
        op("pool", "memset", [], [b_hp], halo_p[:], 0.0)

        ckpt("consts")
        bk = next_bank()
        op("pe", "transpose", [b_vrow, b_id], [b_pm[bk]], pm[bk][:, 0:128], vrow0[:], ident[:])
        op("pe", "transpose", [b_vrow, b_id], [b_pm[bk]], pm[bk][:, 128:NV * 8], vrow1[:], ident[0:NV * 8 - 128, 0:NV * 8 - 128])
        op("dve", "tensor_copy", [b_pm[bk]], [b_vfm], vfm[:], pm[bk][:, 0:NV * 8])
        op("act", "activation", [b_vfm], [b_cst], out=cst[:, C_TMP:C_TMP + 8], in_=vfm[:, V_C:V_C + 8], func=AF.Silu)
        op("dve", "tensor_copy", [b_cst], [b_cst], cactb[:], cst[:, C_TMP:C_TMP + 8])
        ckpt("vfm")
        bkm = next_bank()
        for q in range(12):
            slot, dst = load_piece("pool", "p (k c) -> p k c", dict(k=8),
                                   w_ada.rearrange("(k p) c -> p k c", p=128)[:, :, q * 512:(q + 1) * 512], [])
            for cc in range(4):
                oc = q * 4 + cc
                for k in range(8):
                    op("pe", "matmul", [b_ring[slot], b_cst], [b_pm[bkm]], pm[bkm][:, oc:oc + 1],
                       lhsT=dst[:, k, cc * 128:(cc + 1) * 128], rhs=cactb[:, k:k + 1],
                       start=(k == 0), stop=(k == 7), inc=(k == 7))
        op("dve", "tensor_tensor", [b_pm[bkm], b_vfm], [b_mod], out=modv[:], in0=pm[bkm][:, 0:48], in1=vfm[:, V_BADA:V_BADA + 48], op=ALU.add)
        ckpt("mod")
        R = [b_mod, b_vfm, b_cst]
        W = [b_cst]
        op("dve", "scalar_tensor_tensor", R, W, out=cst[:, C_S1:C_S1 + 8], in0=modv[:, 8:16], scalar=1.0, in1=vfm[:, V_GMIX:V_GMIX + 8], op0=ALU.add, op1=ALU.mult)
        op("dve", "tensor_copy", R, W, cst[:, C_SH1:C_SH1 + 8], modv[:, 0:8])
        op("dve", "scalar_tensor_tensor", R, W, out=cst[:, C_S2:C_S2 + 8], in0=modv[:, 32:40], scalar=1.0, in1=vfm[:, V_GFFN:V_GFFN + 8], op0=ALU.add, op1=ALU.mult)
        op("dve", "tensor_copy", R, W, cst[:, C_SH2:C_SH2 + 8], modv[:, 24:32])
        op("dve", "tensor_scalar", R, W, out=cst[:, C_GT1H:C_GT1H + 8], in0=modv[:, 16:24], scalar1=0.5, scalar2=None, op0=ALU.mult)
        op("dve", "tensor_copy", R, W, cst[:, C_GT2:C_GT2 + 8], modv[:, 40:48])
        op("act", "activation", R, W, out=cst[:, C_TMP:C_TMP + 8], in_=vfm[:, V_LAM:V_LAM + 8], func=AF.Exp, scale=-1.0)
        op("act", "activation", R, W, out=cst[:, C_TMP2:C_TMP2 + 8], in_=cst[:, C_TMP:C_TMP + 8], func=AF.Ln, bias=1.0)
        op("dve", "tensor_scalar", R, W, out=cst[:, C_CC:C_CC + 8], in0=cst[:, C_TMP2:C_TMP2 + 8], scalar1=-8.0, scalar2=None, op0=ALU.mult)
        op("dve", "tensor_scalar", R, W, out=cst[:, C_HC:C_HC + 8], in0=cst[:, C_TMP2:C_TMP2 + 8], scalar1=-4.0, scalar2=None, op0=ALU.mult)
        op("dve", "tensor_scalar", R, W, out=cst[:, C_HBA:C_HBA + 8], in0=vfm[:, V_BRA:V_BRA + 8], scalar1=0.5, scalar2=None, op0=ALU.mult)
        op("dve", "tensor_scalar", R, W, out=cst[:, C_HBX:C_HBX + 8], in0=vfm[:, V_BRX:V_BRX + 8], scalar1=0.5, scalar2=None, op0=ALU.mult)
        ckpt("derived")
        srcs = [(cst, C_GT1H), (cst, C_GT2), (vfm, V_GFIN)]
        n_d = 0
        for m, (tt, base) in enumerate(srcs):
            for j in range(8):
                dsl = n_d % 2
                n_d += 1
                op("dve", "tensor_scalar", [b_id, b_cst, b_vfm], [b_diag[dsl]], out=diag[:, dsl, :], in0=ident[:], scalar1=tt[:, base + j:base + j + 1], scalar2=None, op0=ALU.mult)
                bk = next_bank()
                op("pe", "matmul", [b_ones, b_diag[dsl]], [b_pm[bk]], pm[bk][:, 0:128], lhsT=ones[:], rhs=diag[:, dsl, :], start=True, stop=True)
                op("act", "copy", [b_pm[bk]], [b_bc], bc[:, m, j * 128:(j + 1) * 128], pm[bk][:, 0:128])

        ckpt("bc")
        s_wg = semc("cwg")
        dma("sp", wga[:], wrga_bf.rearrange("h (kk p) c -> p h kk c", p=128), s_wg, [b_s_g], [b_wg])
        dma("sp", wgx[:], wrgx_bf.rearrange("h (kk p) c -> p h kk c", p=128), s_wg, [b_s_g], [b_wg])

        b_s_in = [Buf() for _ in range(7)]
        s_in = [semc(f"cin{f}") for f in range(7)]
        FAM_CB, FAM_CC, FAM_CX, FAM_RX, FAM_RG, FAM_GA, FAM_GB = range(7)

        b_s_out = Buf(); s_out = semc("cout")
        b_s_gu = Buf(); s_gu = semc("cgu")
        b_s_dn = Buf(); s_dn = semc("cdn")
        cast_jobs = []

        def cast_fam(f):
            for r in range(4):
                cast_jobs.append((win_bf[r * 256:(r + 1) * 256, f * D:(f + 1) * D], w_in[r * 256:(r + 1) * 256, f * D:(f + 1) * D], s_in[f], b_s_in[f]))

        for f in (FAM_RX, FAM_CX, FAM_CC, FAM_RG, FAM_GB, FAM_GA, FAM_CB):
            cast_fam(f)
        for r in range(4):
            cast_jobs.append((wout_bf[r * 256:(r + 1) * 256, :], w_out[r * 256:(r + 1) * 256, :], s_out, b_s_out))
        for r in range(4):
            for cq in range(4):
                cast_jobs.append((wgu_bf[r * 256:(r + 1) * 256, cq * 1408:(cq + 1) * 1408], w_gu[r * 256:(r + 1) * 256, cq * 1408:(cq + 1) * 1408], s_gu, b_s_gu))
        for r in range(11):
            cast_jobs.append((wdown_bf[r * 256:(r + 1) * 256, :], w_down[r * 256:(r + 1) * 256, :], s_dn, b_s_dn))

        def issue_casts(n):
            for _ in range(n):
                if cast_jobs:
                    o_, i_, sc_, b_ = cast_jobs.pop(0)
                    dma("pool", o_, i_, sc_, [], [b_], nodeps=True)

        issue_casts(4)
        ckpt("casts")
        s_x = [semc("sx0"), semc("sx1")]
        s_o = [semc("so0"), semc("so1")]

        def load_x(src, t, xb):
            dma("sp", xt[xb][:], src[t * T:(t + 1) * T, :].rearrange("(b p) d -> p b d", p=128), s_x[xb], [], b_xt[xb])

        def norm(xb, kind, blocks=(0, 1, 2, 3), to_xn=True):
            bs = b_stat[kind]
            for b in blocks:
                op("act", "activation", [b_xt[xb][b]], [b_junk, bs[b][0]], out=junk[:], in_=xt[xb][:, b, :], func=AF.Square, accum_out=stat[:, kind, 0, b:b + 1])
            for b in blocks:
                op("dve", "tensor_scalar", [bs[b][0]], [bs[b][1]], out=stat[:, kind, 1, b:b + 1], in0=stat[:, kind, 0, b:b + 1], scalar1=1.0 / D, scalar2=EPS, op0=ALU.mult, op1=ALU.add)
            for b in blocks:
                op("act", "activation", [bs[b][1]], [bs[b][1]], out=stat[:, kind, 1, b:b + 1], in_=stat[:, kind, 1, b:b + 1], func=AF.Sqrt)
            for b in blocks:
                op("dve", "reciprocal", [bs[b][1]], [bs[b][2]], stat[:, kind, 2, b:b + 1], stat[:, kind, 1, b:b + 1])
            if to_xn:
                for b in blocks:
                    op("act", "activation", [b_xt[xb][b], bs[b][2]], [b_xn[b]], out=xn[:, b, :], in_=xt[xb][:, b, :], func=AF.Identity, scale=stat[:, kind, 2, b:b + 1])

        def transposes(c_s, c_sh):
            for j in range(KC):
                bk = next_bank()
                for b in range(NB):
                    op("pe", "transpose", [b_xn[b], b_id], [b_pm[bk]], pm[bk][:, b * 128:(b + 1) * 128], xn[:, b, j * 128:(j + 1) * 128], ident[:], inc=(b == NB - 1))
                op("act", "activation", [b_pm[bk], b_cst], [b_hT[j]], out=hT[:, j, :], in_=pm[bk][:, :], func=AF.Identity, scale=ccol(c_s, j), bias=ccol(c_sh, j))

        def win_piece(fam, g):
            c0 = fam * D + g * 256
            return load_piece("sp", "p (k c) -> p k c", dict(k=8),
                              win_bf.rearrange("(k p) c -> p k c", p=128)[:, :, c0:c0 + 256], [b_s_in[fam]], nelem=2048)

        def proj2(fam, g, consume):
            slot, wv = win_piece(fam, g)
            for l in range(GC):
                bk = next_bank()
                for k in range(KC):
                    op("pe", "matmul", [b_ring[slot], b_hT[k]], [b_pm[bk]], pm[bk][:, :], lhsT=wv[:, k, l * 128:(l + 1) * 128], rhs=hT[:, k, :],
                       start=(k == 0), stop=(k == KC - 1), inc=(k == KC - 1))
                consume(l, bk)

        def lru_part1a(g, rs_=0):
            rxs, b_rxs = rxs_s[rs_], b_rxs_s[rs_]
            j0 = g * GC
            op("dve", "tensor_copy", [b_hrx], [b_rxs[0], b_rxs[1]], rxs[:, :, 0:3], halo_rx[:, j0:j0 + GC, :])

            def c_rx(l, bk):
                op("act", "copy", [b_pm[bk]], [b_rxs[l]], rxs[:, l, 3:T + 3], pm[bk][:, :])
            proj2(FAM_RX, g, c_rx)
            op("dve", "tensor_copy", [b_rxs[0], b_rxs[1]], [b_hrx], halo_rx[:, j0:j0 + GC, :], rxs[:, :, T:T + 3])

        def lru_part1b(g, bs_=0, rs_=0):
            rxs, b_rxs = rxs_s[rs_], b_rxs_s[rs_]
            u, b_u, ubf, b_ubf = u_s[bs_], b_u_s[bs_], ubf_s[bs_], b_ubf_s[bs_]
            j0 = g * GC
            for l in range(GC):
                j = j0 + l
                op("dve", "tensor_scalar", [b_rxs[l], b_vfm], [b_u[l]], out=u[:, l, :], in0=rxs[:, l, 3:T + 3], scalar1=vcol(V_CBW + 24, j), scalar2=vcol(V_CBB, j), op0=ALU.mult, op1=ALU.add)
                for kk in range(1, 4):
                    op("dve", "scalar_tensor_tensor", [b_rxs[l], b_vfm, b_u[l]], [b_u[l]], out=u[:, l, :], in0=rxs[:, l, 3 - kk:T + 3 - kk], scalar=vcol(V_CBW + (3 - kk) * 8, j), in1=u[:, l, :], op0=ALU.mult, op1=ALU.add)
                op("dve", "tensor_copy", [b_u[l]], [b_ubf[l]], ubf[:, l, :], u[:, l, :])

        def lru_part1(g, bs_=0):
            lru_part1a(g, 0)
            lru_part1b(g, bs_, 0)

        def lru_part2(g, first_fix, bs_=0, pool_off=False):
            u, b_u, ubf, b_ubf = u_s[bs_], b_u_s[bs_], ubf_s[bs_], b_ubf_s[bs_]
            j0 = g * GC
            for wsb_, cb_, dst, bdst in ((wga, C_HBA, trt, b_trt), (wgx, C_HBX, TI, b_TI)):
                for l in range(GC):
                    bk = next_bank()
                    for kk in range(2):
                        op("pe", "matmul", [b_wg, b_ubf[kk]], [b_pm[bk]], pm[bk][:, :], lhsT=wsb_[:, g, kk, l * 128:(l + 1) * 128], rhs=ubf[:, kk, :],
                           start=(kk == 0), stop=(kk == 1), inc=(kk == 1))
                    op("act", "activation", [b_pm[bk], b_cst], [bdst[l]], out=dst[:, l, :], in_=pm[bk][:, :], func=AF.Tanh, scale=0.5, bias=ccol(cb_, j0 + l))
            for l in range(GC):
                j = j0 + l
                op("act", "activation", [b_trt[l], b_cst], [b_A[l]], out=Aa[:, l, :], in_=trt[:, l, :], func=AF.Exp, scale=ccol(C_HC, j), bias=ccol(C_HC, j))
                if pool_off:
                    op("pool", "tensor_tensor", [b_A[l]], [b_M[l]], out=Mm[:, l, :], in0=Aa[:, l, :], in1=Aa[:, l, :], op=ALU.mult)
                else:
                    op("act", "activation", [b_trt[l], b_cst], [b_M[l]], out=Mm[:, l, :], in_=trt[:, l, :], func=AF.Exp, scale=ccol(C_CC, j), bias=ccol(C_CC, j))
            for l in range(GC):
                op("act", "activation", [b_M[l]], [b_M[l]], out=Mm[:, l, :], in_=Mm[:, l, :], func=AF.Sqrt, scale=-0.25 * (1.0 - 2.0 ** -20), bias=0.25)
            for l in range(GC):
                j = j0 + l
                if first_fix == "half":
                    op("dve", "memset", [], [b_M[l]], Mm[:, l, 0:1], 0.5)
                elif first_fix == "flag":
                    op("dve", "tensor_scalar", [b_M[l], b_flg], [b_M[l]], out=Mm[:, l, 0:1], in0=Mm[:, l, 0:1], scalar1=flg[:, 1:2], scalar2=None, op0=ALU.max)
                op("dve", "scalar_tensor_tensor", [b_TI[l], b_u[l]], [b_TI[l]], out=TI[:, l, :], in0=TI[:, l, :], scalar=1.0, in1=u[:, l, :], op0=ALU.add, op1=ALU.mult)
                op("pool" if pool_off else "dve", "tensor_tensor", [b_TI[l], b_M[l]], [b_M[l]], out=Mm[:, l, :], in0=TI[:, l, :], in1=Mm[:, l, :], op=ALU.mult)
                op("dve", "tensor_tensor_scan", [b_A[l], b_M[l], b_hs], [b_hb[l]], out=hb[:, l, :], data0=Aa[:, l, :], data1=Mm[:, l, :], initial=hstate[:, j:j + 1], op0=ALU.mult, op1=ALU.add)
                op("dve", "tensor_copy", [b_hb[l]], [b_hs], hstate[:, j:j + 1], hb[:, l, T - 1:T])

        def p_group(g):
            j0 = g * GC
            op("dve", "tensor_copy", [b_hp], [b_pp[0], b_pp[1]], pp[:, :, 0:2], halo_p[:, j0:j0 + GC, :])

            def c_cx(l, bk):
                op("act", "copy", [b_pm[bk]], [b_cxs[l]], cxs[:, l, :], pm[bk][:, :])
            proj2(FAM_CX, g, c_cx)

            def c_cc(l, bk):
                op("dve", "tensor_tensor", [b_pm[bk], b_cxs[l]], [b_pp[l]], out=pp[:, l, 2:T + 2], in0=pm[bk][:, :], in1=cxs[:, l, :], op=ALU.mult)
            proj2(FAM_CC, g, c_cc)
            op("dve", "tensor_copy", [b_pp[0], b_pp[1]], [b_hp], halo_p[:, j0:j0 + GC, :], pp[:, :, T:T + 2])

        def mixer_group(g, first):
            j0 = g * GC
            lru_part1(g)

            def c_rg(l, bk):
                op("act", "activation", [b_pm[bk]], [b_gl[l]], out=gl[:, l, :], in_=pm[bk][:, :], func=AF.Gelu_apprx_tanh)
            proj2(FAM_RG, g, c_rg)

            def c_gb(l, bk):
                op("act", "activation", [b_pm[bk]], [b_tb[l]], out=tb[:, l, :], in_=pm[bk][:, :], func=AF.Tanh, scale=0.5)
            proj2(FAM_GB, g, c_gb)

            def c_ga(l, bk):
                op("act", "activation", [b_pm[bk]], [b_ta[l]], out=ta[:, l, :], in_=pm[bk][:, :], func=AF.Tanh, scale=0.5)
            proj2(FAM_GA, g, c_ga)
            p_group(g)
            for l in range(GC):
                j = j0 + l
                op("dve", "tensor_scalar", [b_pp[l], b_vfm], [b_ya[l]], out=ya[:, l, :], in0=pp[:, l, 2:T + 2], scalar1=vcol(V_CAW + 16, j), scalar2=None, op0=ALU.mult)
                for kk in range(1, 3):
                    op("dve", "scalar_tensor_tensor", [b_pp[l], b_vfm, b_ya[l]], [b_ya[l]], out=ya[:, l, :], in0=pp[:, l, 2 - kk:T + 2 - kk], scalar=vcol(V_CAW + (2 - kk) * 8, j), in1=ya[:, l, :], op0=ALU.mult, op1=ALU.add)
            lru_part2(g, "flag" if first else None)
            for l in range(GC):
                op("dve", "tensor_tensor", [b_hb[l], b_gl[l]], [b_hb[l]], out=hb[:, l, :], in0=hb[:, l, :], in1=gl[:, l, :], op=ALU.mult)
                op("dve", "scalar_tensor_tensor", [b_tb[l], b_hb[l]], [b_hb[l]], out=hb[:, l, :], in0=tb[:, l, :], scalar=1.0, in1=hb[:, l, :], op0=ALU.add, op1=ALU.mult)

            def c_cb(l, bk):
                j = j0 + l
                op("dve", "tensor_tensor", [b_pm[bk], b_ya[l]], [b_ya[l]], out=ya[:, l, :], in0=pm[bk][:, :], in1=ya[:, l, :], op=ALU.mult)
                op("dve", "scalar_tensor_tensor", [b_ta[l], b_ya[l]], [b_ya[l]], out=ya[:, l, :], in0=ta[:, l, :], scalar=1.0, in1=ya[:, l, :], op0=ALU.add, op1=ALU.mult)
                op("dve", "tensor_tensor", [b_ya[l], b_hb[l]], [b_mT[j]], out=mT[:, j, :], in0=ya[:, l, :], in1=hb[:, l, :], op=ALU.add)
            proj2(FAM_CB, g, c_cb)

        def resid_add(xb, b, hf, bk, bcm):
            sl = (b * 2 + hf) % 2
            op("dve", "tensor_tensor", [b_pm[bk], b_bc], [b_tmpg[sl]], out=tmpg[:, sl, :], in0=pm[bk][:, :], in1=bc[:, bcm, hf * T:(hf + 1) * T], op=ALU.mult)
            op("pool", "tensor_tensor", [b_tmpg[sl], b_xt[xb][b]], [b_xt[xb][b]], out=xt[xb][:, b, hf * T:(hf + 1) * T], in0=tmpg[:, sl, :], in1=xt[xb][:, b, hf * T:(hf + 1) * T], op=ALU.add)

        def out_proj(xb):
            pcs = []
            for hf in range(2):
                pcs.append(load_piece("sp", "p (k c) -> p k c", dict(k=8), wout_bf.rearrange("(k p) c -> p k c", p=128)[:, :, hf * T:(hf + 1) * T], [b_s_out]))
            bks = {}
            JA = KC - GC
            for b in range(NB):
                for hf in range(2):
                    slot, wv = pcs[hf]
                    bk = next_bank()
                    bks[(b, hf)] = bk
                    for j in range(JA):
                        op("pe", "matmul", [b_ring[slot], b_mT[j]], [b_pm[bk]], pm[bk][:, :], lhsT=mT[:, j, b * 128:(b + 1) * 128], rhs=wv[:, j, :],
                           start=(j == 0), stop=False, inc=False)
            for b in range(NB):
                for hf in range(2):
                    slot, wv = pcs[hf]
                    bk = bks[(b, hf)]
                    for j in range(JA, KC):
                        op("pe", "matmul", [b_ring[slot], b_mT[j]], [b_pm[bk]], pm[bk][:, :], lhsT=mT[:, j, b * 128:(b + 1) * 128], rhs=wv[:, j, :],
                           start=False, stop=(j == KC - 1), inc=(j == KC - 1))
                    resid_add(xb, b, hf, bk, 0)
                norm(xb, 1, [b])

        def ffn_up():
            gu = wgu_bf.rearrange("(k p) (two f) -> p k two f", p=128, two=2)
            for q in range(NF // 2):
                slot = st["slot"]
                st["slot"] = (slot + 1) % NSLOT
                wv = ring[:, slot, :].rearrange("p (k two c) -> p k two c", k=8, two=2)
                dma("sp", wv[:, :, 0, :], gu[:, :, 0, q * 256:(q + 1) * 256], ring_sem[slot], [b_s_gu], [b_ring[slot]])
                dma("sp", wv[:, :, 1, :], gu[:, :, 1, q * 256:(q + 1) * 256], ring_sem[slot], [b_s_gu], [b_ring[slot]], nodeps=True)
                for l in range(2):
                    f = q * 2 + l
                    bkg = next_bank()
                    for k in range(KC):
                        op("pe", "matmul", [b_ring[slot], b_hT[k]], [b_pm[bkg]], pm[bkg][:, :], lhsT=wv[:, k, 0, l * 128:(l + 1) * 128], rhs=hT[:, k, :],
                           start=(k == 0), stop=(k == KC - 1), inc=(k == KC - 1))
                    bku = next_bank()
                    for k in range(KC):
                        op("pe", "matmul", [b_ring[slot], b_hT[k]], [b_pm[bku]], pm[bku][:, :], lhsT=wv[:, k, 1, l * 128:(l + 1) * 128], rhs=hT[:, k, :],
                           start=(k == 0), stop=(k == KC - 1), inc=(k == KC - 1))
                    sl = f % 2
                    op("act", "activation", [b_pm[bkg]], [b_sg[sl]], out=sg[:, sl, :], in_=pm[bkg][:, :], func=AF.Silu)
                    op("dve", "tensor_tensor", [b_pm[bku], b_sg[sl]], [b_actT[f]], out=actT[:, f, :], in0=pm[bku][:, :], in1=sg[:, sl, :], op=ALU.mult)

        def ffn_down(xb):
            dn = wdown_bf.rearrange("(f p) c -> p f c", p=128)
            groups = [(0, 8), (8, 8), (16, 6)]
            for hf in range(2):
                bks = [next_bank() for _ in range(NB)]
                for (f0, nf) in groups:
                    slot, wv = load_piece("sp", "p (f c) -> p f c", dict(f=nf), dn[:, f0:f0 + nf, hf * T:(hf + 1) * T], [b_s_dn], nelem=nf * T)
                    for b in range(NB):
                        for fl in range(nf):
                            f = f0 + fl
                            op("pe", "matmul", [b_ring[slot], b_actT[f]], [b_pm[bks[b]]], pm[bks[b]][:, :], lhsT=actT[:, f, b * 128:(b + 1) * 128], rhs=wv[:, fl, :],
                               start=(f == 0), stop=(f == NF - 1), inc=(fl == nf - 1))
                for b in range(NB):
                    resid_add(xb, b, hf, bks[b], 1)

        def final_norm_store(xb, t):
            norm(xb, 2, to_xn=False)
            rs = stat[:, 2, 2, :]
            for b in range(NB):
                op("dve", "scalar_tensor_tensor", [b_xt[xb][b], b_stat[2][b][2], b_bc], [b_xt[xb][b]], out=xt[xb][:, b, :], in0=xt[xb][:, b, :], scalar=rs[:, b:b + 1], in1=bc[:, 2, :], op0=ALU.mult, op1=ALU.mult)
            dma("pool", out[t * T:(t + 1) * T, :].rearrange("(b p) d -> p b d", p=128), xt[xb][:], s_o[xb], b_xt[xb], [])

        seq = [(t, g) for t in range(NT) for g in range(KC // GC)]
        load_x(x_prev, 0, 0)

        def stage1a(i):
            t, g = seq[i]
            if g == 0:
                if t + 1 < NT:
                    load_x(x_prev, t + 1, (t + 1) % 2)
                else:
                    load_x(x_main, 0, (t + 1) % 2)
                norm(t % 2, 0)
                transposes(C_S1, C_SH1)
            issue_casts(2)
            lru_part1a(g, i % 2)

        def stage1b(i):
            t, g = seq[i]
            lru_part1b(g, i % 2, i % 2)

        def stage2(i):
            t, g = seq[i]
            lru_part2(g, "half" if t == 0 else None, i % 2, pool_off=False)
            if t == NT - 1:
                p_group(g)
            ckpt(f"pre{t}.g{g}")

        NS_ = len(seq)
        stage1a(0)
        stage1a(1)
        stage1b(0)
        for i in range(NS_):
            if i + 2 < NS_:
                stage1a(i + 2)
            if i + 1 < NS_:
                stage1b(i + 1)
            stage2(i)
        nx = NT
        issue_casts(1000)
        op("dve", "tensor_scalar", [b_hs, b_flg], [b_hs], out=hstate[:], in0=hstate[:], scalar1=flg[:, 0:1], scalar2=None, op0=ALU.mult)
        op("dve", "tensor_scalar", [b_hrx, b_flg], [b_hrx], out=halo_rx[:], in0=halo_rx[:], scalar1=flg[:, 0:1], scalar2=None, op0=ALU.mult)
        op("dve", "tensor_scalar", [b_hp, b_flg], [b_hp], out=halo_p[:], in0=halo_p[:], scalar1=flg[:, 0:1], scalar2=None, op0=ALU.mult)

        norm(nx % 2, 0)
        transposes(C_S1, C_SH1)
        for t in range(NT):
            xb = nx % 2
            ckpt(f"main{t}.tr")
            for g in range(KC // GC):
                mixer_group(g, t == 0)
                ckpt(f"main{t}.g{g}")
            if t + 1 < NT:
                load_x(x_main, t + 1, (nx + 1) % 2)
            out_proj(xb)
            ckpt(f"main{t}.outproj")
            transposes(C_S2, C_SH2)
            ckpt(f"main{t}.tr2")
            ffn_up()
            ckpt(f"main{t}.ffnup")
            if t + 1 < NT:
                norm((nx + 1) % 2, 0)
            ffn_down(xb)
            ckpt(f"main{t}.ffndown")
            if t + 1 < NT:
                transposes(C_S1, C_SH1)
            final_norm_store(xb, t)
            ckpt(f"main{t}.store")
            nx += 1
        for sc_ in s_o:
            sem_, val_ = sc_
            if val_:
                S.E["pool"]["prog"].append(lambda sem_=sem_, val_=val_: nc.gpsimd.wait_ge(sem_, val_))

        with nc.Block() as block:
            @block.tensor
            def _(e):
                for f in S.E["pe"]["prog"]:
                    f()

            @block.scalar
            def _(e):
                for f in S.E["act"]["prog"]:
                    f()

            @block.vector
            def _(e):
                for f in S.E["dve"]["prog"]:
                    f()

            @block.gpsimd
            def _(e):
                for f in S.E["pool"]["prog"]:
                    f()

            @block.sync
            def _(e):
                for f in S.E["sp"]["prog"]:
                    f()
    return nc


def kernel(x, c, w_ada, b_ada, g_norm_mix, w_in, conv_a_w, conv_b_w, conv_b_bias,
           w_rg_a, b_rg_a, w_rg_x, b_rg_x, lru_lambda, w_out, g_norm_ffn,
           w_gate_up, w_down, g_norm_final, _return_maps=False):
    f32 = np.float32
    x = np.asarray(x, f32)
    ca = lambda a: np.ascontiguousarray(np.asarray(a, f32))
    shared = dict(
        w_ada=ca(w_ada[0]), w_in=ca(w_in[0]), w_rg_a=ca(w_rg_a[0]), w_rg_x=ca(w_rg_x[0]),
        w_out=ca(w_out[0]), w_gu=ca(w_gate_up[0]), w_down=ca(w_down[0]),
    )
    in_maps = []
    for i in range(NCORES):
        b, h = divmod(i, 2)
        rows = [np.asarray(c, f32)[b]] + [np.asarray(b_ada, f32)[0].reshape(6, D)[m] for m in range(6)]
        rows += [np.asarray(g_norm_mix, f32)[0]]
        rows += [np.asarray(conv_a_w, f32)[0][k] for k in range(3)]
        rows += [np.asarray(conv_b_w, f32)[0][k] for k in range(4)]
        rows += [np.asarray(conv_b_bias, f32)[0], np.asarray(b_rg_a, f32)[0], np.asarray(b_rg_x, f32)[0],
                 np.asarray(lru_lambda, f32)[0], np.asarray(g_norm_ffn, f32)[0], np.asarray(g_norm_final, f32)]
        vecs = np.ascontiguousarray(np.stack(rows, 0).reshape(NV * 8, 128))
        flags = np.zeros((128, 2), f32)
        flags[:, 0] = 1.0 if h == 1 else 0.0
        flags[:, 1] = 0.5 if h == 0 else 0.0
        xm = np.ascontiguousarray(x[b, h * TOK:(h + 1) * TOK])
        xp = np.ascontiguousarray(x[b, 0:TOK])
        m = dict(shared)
        m.update(x_main=xm, x_prev=xp, vecs=vecs, flags=flags)
        in_maps.append(m)
    if _return_maps:
        return in_maps
    nc = build_nc()
    res = run_bass_kernel_spmd(nc, in_maps, core_ids=list(range(NCORES)))
    outp = np.empty((4, 2 * TOK, D), f32)
    for i in range(NCORES):
        b, h = divmod(i, 2)
        outp[b, h * TOK:(h + 1) * TOK] = res.results[i]["out"]
    return outp
```

```python
import numpy as np
from contextlib import ExitStack
import concourse.bass as bass
import concourse.mybir as mybir
from concourse.bass_utils import run_bass_kernel_spmd

F32 = mybir.dt.float32
BF16 = mybir.dt.bfloat16
AF = mybir.ActivationFunctionType
ALU = mybir.AluOpType

D = 1024
KC = 8
T = 512
NB = 4
NT = 8
FF = 2816
NF = 22
NCORES = 8
TOK = 4096
EPS = 1e-6
NSLOT = 4
NV = 21


class Buf:
    __slots__ = ("w", "r")

    def __init__(self):
        self.w = None
        self.r = {}


class Sched:
    def __init__(self):
        self.E = {}

    def add_engine(self, name, eng, sem):
        self.E[name] = dict(name=name, eng=eng, sem=sem, count=0, prog=[], waited={})

    def _wait(self, E, tok):
        sem, val, owner = tok
        if owner == "pe" and E["name"] == "pe":
            return
        key = id(sem)
        if E["waited"].get(key, 0) >= val:
            return
        E["waited"][key] = val
        eng = E["eng"]
        E["prog"].append(lambda: eng.wait_ge(sem, val))

    def _deps(self, E, reads, writes):
        for b in reads:
            if b.w is not None:
                self._wait(E, b.w)
        for b in writes:
            if b.w is not None:
                self._wait(E, b.w)
            for t in b.r.values():
                self._wait(E, t)

    @staticmethod
    def _upd(tok, reads, writes):
        for b in writes:
            b.w = tok
            b.r = {}
        for b in reads:
            if b.w is tok:
                continue
            b.r[id(tok[0])] = tok

    def op(self, ename, fn, reads=(), writes=(), inc=True):
        E = self.E[ename]
        self._deps(E, reads, writes)
        seq = E["count"] + 1
        sem = E["sem"]
        if inc:
            E["count"] = seq
            E["prog"].append(lambda: fn().then_inc(sem, 1))
        else:
            E["prog"].append(fn)
        self._upd((sem, seq, ename), reads, writes)

    def dma(self, qname, fn, semc, reads=(), writes=(), nodeps=False):
        E = self.E[qname]
        if not nodeps:
            self._deps(E, reads, writes)
        semc[1] += 16
        sem = semc[0]
        E["prog"].append(lambda: fn().then_inc(sem, 16))
        self._upd((sem, semc[1], "dma"), reads, writes)


class _Stop(Exception):
    pass


def build_nc(limit=None, dbg=None):
    nc = bass.Bass("TRN2", target_bir_lowering=False)
    _ck = [0]

    def ckpt(name):
        _ck[0] += 1
        if limit is not None and _ck[0] == limit:
            print("STOP at ckpt", _ck[0], name)
            _ck.append("stopped")

    def stopped():
        return len(_ck) > 1

    def din(name, shape, dt=F32):
        return nc.dram_tensor(name, shape, dt, kind="ExternalInput").ap()

    x_main = din("x_main", [TOK, D])
    x_prev = din("x_prev", [TOK, D])
    vecs = din("vecs", [NV * 8, 128])
    flags = din("flags", [128, 2])
    w_ada = din("w_ada", [D, 6 * D])
    w_in = din("w_in", [D, 7 * D])
    w_rg_a = din("w_rg_a", [4, 256, 256])
    w_rg_x = din("w_rg_x", [4, 256, 256])
    w_out = din("w_out", [D, D])
    w_gu = din("w_gu", [D, 2 * FF])
    w_down = din("w_down", [FF, D])
    out = nc.dram_tensor("out", [TOK, D], F32, kind="ExternalOutput").ap()

    def dscr(name, shape):
        return nc.dram_tensor(name, shape, BF16, kind="Internal").ap()

    win_bf = dscr("win_bf", [D, 7 * D])
    wgu_bf = dscr("wgu_bf", [D, 2 * FF])
    wdown_bf = dscr("wdown_bf", [FF, D])
    wout_bf = dscr("wout_bf", [D, D])
    wrga_bf = dscr("wrga_bf", [4, 256, 256])
    wrgx_bf = dscr("wrgx_bf", [4, 256, 256])

    with ExitStack() as es:
        def sb(name, shape, dt=F32):
            return es.enter_context(nc.sbuf_tensor(name, shape, dt))

        def ps(name, shape, dt=F32):
            return es.enter_context(nc.psum_tensor(name, shape, dt))

        def semc(name):
            return [es.enter_context(nc.semaphore(name)), 0]

        S = Sched()
        ENG = dict(pe=nc.tensor, act=nc.scalar, dve=nc.vector, pool=nc.gpsimd, sp=nc.sync)
        for nm, e in ENG.items():
            S.add_engine(nm, e, es.enter_context(nc.semaphore("s_" + nm)))

        def op(e, meth, reads, writes, *args, inc=True, **kw):
            if stopped():
                return
            eng = ENG[e]
            S.op(e, lambda: getattr(eng, meth)(*args, **kw), reads, writes, inc)

        def dma(q, out_ap, in_ap, sc, reads, writes, nodeps=False):
            if stopped():
                return
            eng = ENG[q]
            S.dma(q, lambda: eng.dma_start(out=out_ap, in_=in_ap), sc, reads, writes, nodeps)

        xt = [sb(f"xt{i}", [128, NB, D]) for i in range(2)]
        b_xt = [[Buf() for _ in range(NB)] for _ in range(2)]
        xn = sb("xn", [128, NB, D]); b_xn = [Buf() for _ in range(NB)]
        junk = sb("junk", [128, D], BF16); b_junk = Buf()
        hT = sb("hT", [128, KC, T], BF16); b_hT = [Buf() for _ in range(KC)]
        ring = sb("ring", [128, NSLOT, 4096], BF16); b_ring = [Buf() for _ in range(NSLOT)]
        ring_sem = [semc(f"rs{i}") for i in range(NSLOT)]
        ring_sem_sw = [semc(f"rw{i}") for i in range(NSLOT)]
        wga = sb("wga", [128, 4, 2, 256], BF16)
        wgx = sb("wgx", [128, 4, 2, 256], BF16)
        b_wg = Buf()
        vrow0 = sb("vrow0", [128, 128]); vrow1 = sb("vrow1", [NV * 8 - 128, 128]); b_vrow = Buf()
        vfm = sb("vfm", [128, NV * 8]); b_vfm = Buf()
        ident = sb("ident", [128, 128]); identb = sb("identb", [128, 128], BF16); b_id = Buf()
        ones = sb("ones", [128, 128]); b_ones = Buf()
        cst = sb("cst", [128, 16 * 8]); b_cst = Buf()
        cactb = sb("cactb", [128, 8], BF16)
        modv = sb("modv", [128, 48]); b_mod = Buf()
        flg = sb("flg", [128, 2]); b_flg = Buf()
        bc = sb("bc", [128, 3, D]); b_bc = Buf()
        diag = sb("diag", [128, 2, 128]); b_diag = [Buf(), Buf()]
        stat = sb("stat", [128, 3, 3, NB]); b_stat = [[[Buf() for _ in range(3)] for _ in range(NB)] for _ in range(3)]
        cm05 = sb("cm05", [128, NB]); b_cm05 = Buf()
        hstate = sb("hstate", [128, KC]); b_hs = Buf()
        halo_rx = sb("halo_rx", [128, KC, 3]); b_hrx = Buf()
        halo_p = sb("halo_p", [128, KC, 2]); b_hp = Buf()
        GC = 2
        rxs_s = [sb(f"rxs{i}", [128, GC, T + 3]) for i in range(2)]; b_rxs_s = [[Buf() for _ in range(GC)] for _ in range(2)]
        u_s = [sb(f"u{i}", [128, GC, T]) for i in range(2)]; b_u_s = [[Buf() for _ in range(GC)] for _ in range(2)]
        ubf_s = [sb(f"ubf{i}", [128, GC, T], BF16) for i in range(2)]; b_ubf_s = [[Buf() for _ in range(GC)] for _ in range(2)]
        Aa = sb("Aa", [128, GC, T]); b_A = [Buf() for _ in range(GC)]
        Mm = sb("Mm", [128, GC, T]); b_M = [Buf() for _ in range(GC)]
        TI = sb("TI", [128, GC, T]); b_TI = [Buf() for _ in range(GC)]
        trt = sb("trt", [128, GC, T]); b_trt = [Buf() for _ in range(GC)]
        hb = sb("hb", [128, GC, T]); b_hb = [Buf() for _ in range(GC)]
        gl = sb("gl", [128, GC, T], BF16); b_gl = [Buf() for _ in range(GC)]
        tb = sb("tb", [128, GC, T]); b_tb = [Buf() for _ in range(GC)]
        ta = sb("ta", [128, GC, T]); b_ta = [Buf() for _ in range(GC)]
        cxs = sb("cxs", [128, GC, T]); b_cxs = [Buf() for _ in range(GC)]
        pp = sb("pp", [128, GC, T + 2]); b_pp = [Buf() for _ in range(GC)]
        ya = sb("ya", [128, GC, T]); b_ya = [Buf() for _ in range(GC)]
        actT = sb("actT", [128, NF, T], BF16); b_actT = [Buf() for _ in range(NF)]
        mT = actT; b_mT = b_actT
        sg = sb("sg", [128, 2, T]); b_sg = [Buf(), Buf()]
        tmpg = sb("tmpg", [128, 2, T]); b_tmpg = [Buf(), Buf()]
        NPM = 8
        pm = [ps(f"pm{i}", [128, T]) for i in range(NPM)]; b_pm = [Buf() for _ in range(NPM)]
        st = dict(bank=0, slot=0, tp=0)

        def next_bank():
            st["bank"] = (st["bank"] + 1) % NPM
            return st["bank"]

        C_S1, C_SH1, C_S2, C_SH2, C_GT1H, C_GT2, C_HC, C_CC, C_HBA, C_HBX, C_TMP, C_TMP2 = [i * 8 for i in range(12)]
        V_C, V_BADA, V_GMIX, V_CAW, V_CBW, V_CBB, V_BRA, V_BRX, V_LAM, V_GFFN, V_GFIN = 0, 8, 56, 64, 88, 120, 128, 136, 144, 152, 160

        def vcol(base, j):
            return vfm[:, base + j:base + j + 1]

        def ccol(base, j):
            return cst[:, base + j:base + j + 1]

        b_s_g = Buf(); s_g = semc("cg")
        dma("pool", wrga_bf, w_rg_a, s_g, [], [b_s_g])
        dma("pool", wrgx_bf, w_rg_x, s_g, [], [b_s_g])
        s_small = semc("csm")
        vv = vecs
        dma("sp", vrow0[:], vv[0:128, :], s_small, [], [b_vrow])
        dma("sp", vrow1[:], vv[128:NV * 8, :], s_small, [], [b_vrow])
        s_flg = semc("cflg")
        dma("sp", flg[:], flags, s_flg, [], [b_flg])
        ada_slots = []

        def ring_view(slot, pat, **kw):
            return ring[:, slot, :].rearrange(pat, **kw)

        def load_piece(q, view_pat, view_kw, src, reads, nelem=4096):
            slot = st["slot"]
            st["slot"] = (slot + 1) % NSLOT
            dst = ring[:, slot, 0:nelem].rearrange(view_pat, **view_kw)
            dma(q, dst, src, (ring_sem_sw if q == "pool" else ring_sem)[slot], reads, [b_ring[slot]])
            return slot, dst

        op("pool", "memset", [], [b_id], ident[:], 0.0)
        op("pool", "affine_select", [b_id], [b_id], out=ident[:], in_=ident[:], compare_op=ALU.not_equal,
           fill=1.0, base=0, pattern=[[-1, 128]], channel_multiplier=1)
        op("dve", "tensor_copy", [b_id], [b_id], identb[:], ident[:])
        op("pool", "memset", [], [b_ones], ones[:], 1.0)
        op("pool", "memset", [], [b_cm05], cm05[:], -0.5)
        op("pool", "memset", [], [b_hs], hstate[:], 0.0)
        op("pool", "memset", [], [b_hrx], halo_rx[:], 0.0)
        op("pool", "memset", [], [b_hp], halo_p[:], 0.0)

        ckpt("consts")
        bk = next_bank()
        op("pe", "transpose", [b_vrow, b_id], [b_pm[bk]], pm[bk][:, 0:128], vrow0[:], ident[:])
        op("pe", "transpose", [b_vrow, b_id], [b_pm[bk]], pm[bk][:, 128:NV * 8], vrow1[:], ident[0:NV * 8 - 128, 0:NV * 8 - 128])
        op("dve", "tensor_copy", [b_pm[bk]], [b_vfm], vfm[:], pm[bk][:, 0:NV * 8])
        op("act", "activation", [b_vfm], [b_cst], out=cst[:, C_TMP:C_TMP + 8], in_=vfm[:, V_C:V_C + 8], func=AF.Silu)
        op("dve", "tensor_copy", [b_cst], [b_cst], cactb[:], cst[:, C_TMP:C_TMP + 8])
        ckpt("vfm")
        bkm = next_bank()
        for q in range(12):
            slot, dst = load_piece("pool", "p (k c) -> p k c", dict(k=8),
                                   w_ada.rearrange("(k p) c -> p k c", p=128)[:, :, q * 512:(q + 1) * 512], [])
            for cc in range(4):
                oc = q * 4 + cc
                for k in range(8):
                    op("pe", "matmul", [b_ring[slot], b_cst], [b_pm[bkm]], pm[bkm][:, oc:oc + 1],
                       lhsT=dst[:, k, cc * 128:(cc + 1) * 128], rhs=cactb[:, k:k + 1],
                       start=(k == 0), stop=(k == 7), inc=(k == 7))
        op("dve", "tensor_tensor", [b_pm[bkm], b_vfm], [b_mod], out=modv[:], in0=pm[bkm][:, 0:48], in1=vfm[:, V_BADA:V_BADA + 48], op=ALU.add)
        ckpt("mod")
        R = [b_mod, b_vfm, b_cst]
        W = [b_cst]
        op("dve", "scalar_tensor_tensor", R, W, out=cst[:, C_S1:C_S1 + 8], in0=modv[:, 8:16], scalar=1.0, in1=vfm[:, V_GMIX:V_GMIX + 8], op0=ALU.add, op1=ALU.mult)
        op("dve", "tensor_copy", R, W, cst[:, C_SH1:C_SH1 + 8], modv[:, 0:8])
        op("dve", "scalar_tensor_tensor", R, W, out=cst[:, C_S2:C_S2 + 8], in0=modv[:, 32:40], scalar=1.0, in1=vfm[:, V_GFFN:V_GFFN + 8], op0=ALU.add, op1=ALU.mult)
        op("dve", "tensor_copy", R, W, cst[:, C_SH2:C_SH2 + 8], modv[:, 24:32])
        op("dve", "tensor_scalar", R, W, out=cst[:, C_GT1H:C_GT1H + 8], in0=modv[:, 16:24], scalar1=0.5, scalar2=None, op0=ALU.mult)
        op("dve", "tensor_copy", R, W, cst[:, C_GT2:C_GT2 + 8], modv[:, 40:48])
        op("act", "activation", R, W, out=cst[:, C_TMP:C_TMP + 8], in_=vfm[:, V_LAM:V_LAM + 8], func=AF.Exp, scale=-1.0)
        op("act", "activation", R, W, out=cst[:, C_TMP2:C_TMP2 + 8], in_=cst[:, C_TMP:C_TMP + 8], func=AF.Ln, bias=1.0)
        op("dve", "tensor_scalar", R, W, out=cst[:, C_CC:C_CC + 8], in0=cst[:, C_TMP2:C_TMP2 + 8], scalar1=-8.0, scalar2=None, op0=ALU.mult)
        op("dve", "tensor_scalar", R, W, out=cst[:, C_HC:C_HC + 8], in0=cst[:, C_TMP2:C_TMP2 + 8], scalar1=-4.0, scalar2=None, op0=ALU.mult)
        op("dve", "tensor_scalar", R, W, out=cst[:, C_HBA:C_HBA + 8], in0=vfm[:, V_BRA:V_BRA + 8], scalar1=0.5, scalar2=None, op0=ALU.mult)
        op("dve", "tensor_scalar", R, W, out=cst[:, C_HBX:C_HBX + 8], in0=vfm[:, V_BRX:V_BRX + 8], scalar1=0.5, scalar2=None, op0=ALU.mult)
        ckpt("derived")
        srcs = [(cst, C_GT1H), (cst, C_GT2), (vfm, V_GFIN)]
        n_d = 0
        for m, (tt, base) in enumerate(srcs):
            for j in range(8):
                dsl = n_d % 2
                n_d += 1
                op("dve", "tensor_scalar", [b_id, b_cst, b_vfm], [b_diag[dsl]], out=diag[:, dsl, :], in0=ident[:], scalar1=tt[:, base + j:base + j + 1], scalar2=None, op0=ALU.mult)
                bk = next_bank()
                op("pe", "matmul", [b_ones, b_diag[dsl]], [b_pm[bk]], pm[bk][:, 0:128], lhsT=ones[:], rhs=diag[:, dsl, :], start=True, stop=True)
                op("act", "copy", [b_pm[bk]], [b_bc], bc[:, m, j * 128:(j + 1) * 128], pm[bk][:, 0:128])

        ckpt("bc")
        s_wg = semc("cwg")
        dma("sp", wga[:], wrga_bf.rearrange("h (kk p) c -> p h kk c", p=128), s_wg, [b_s_g], [b_wg])
        dma("sp", wgx[:], wrgx_bf.rearrange("h (kk p) c -> p h kk c", p=128), s_wg, [b_s_g], [b_wg])

        b_s_in = [Buf() for _ in range(7)]
        s_in = [semc(f"cin{f}") for f in range(7)]
        FAM_CB, FAM_CC, FAM_CX, FAM_RX, FAM_RG, FAM_GA, FAM_GB = range(7)

        b_s_out = Buf(); s_out = semc("cout")
        b_s_gu = Buf(); s_gu = semc("cgu")
        b_s_dn = Buf(); s_dn = semc("cdn")
        cast_jobs = []

        def cast_fam(f):
            for r in range(4):
                cast_jobs.append((win_bf[r * 256:(r + 1) * 256, f * D:(f + 1) * D], w_in[r * 256:(r + 1) * 256, f * D:(f + 1) * D], s_in[f], b_s_in[f]))

        for f in (FAM_RX, FAM_CX, FAM_CC, FAM_RG, FAM_GB, FAM_GA, FAM_CB):
            cast_fam(f)
        for r in range(4):
            cast_jobs.append((wout_bf[r * 256:(r + 1) * 256, :], w_out[r * 256:(r + 1) * 256, :], s_out, b_s_out))
        for r in range(4):
            for cq in range(4):
                cast_jobs.append((wgu_bf[r * 256:(r + 1) * 256, cq * 1408:(cq + 1) * 1408], w_gu[r * 256:(r + 1) * 256, cq * 1408:(cq + 1) * 1408], s_gu, b_s_gu))
        for r in range(11):
            cast_jobs.append((wdown_bf[r * 256:(r + 1) * 256, :], w_down[r * 256:(r + 1) * 256, :], s_dn, b_s_dn))

        def issue_casts(n):
            for _ in range(n):
                if cast_jobs:
                    o_, i_, sc_, b_ = cast_jobs.pop(0)
                    dma("pool", o_, i_, sc_, [], [b_], nodeps=True)

        issue_casts(4)
        ckpt("casts")
        s_x = [semc("sx0"), semc("sx1")]
        s_o = [semc("so0"), semc("so1")]

        def load_x(src, t, xb):
            dma("sp", xt[xb][:], src[t * T:(t + 1) * T, :].rearrange("(b p) d -> p b d", p=128), s_x[xb], [], b_xt[xb])

        def norm(xb, kind, blocks=(0, 1, 2, 3), to_xn=True):
            bs = b_stat[kind]
            for b in blocks:
                op("act", "activation", [b_xt[xb][b]], [b_junk, bs[b][0]], out=junk[:], in_=xt[xb][:, b, :], func=AF.Square, accum_out=stat[:, kind, 0, b:b + 1])
            for b in blocks:
                op("dve", "tensor_scalar", [bs[b][0]], [bs[b][1]], out=stat[:, kind, 1, b:b + 1], in0=stat[:, kind, 0, b:b + 1], scalar1=1.0 / D, scalar2=EPS, op0=ALU.mult, op1=ALU.add)
            for b in blocks:
                op("act", "activation", [bs[b][1]], [bs[b][1]], out=stat[:, kind, 1, b:b + 1], in_=stat[:, kind, 1, b:b + 1], func=AF.Sqrt)
            for b in blocks:
                op("dve", "reciprocal", [bs[b][1]], [bs[b][2]], stat[:, kind, 2, b:b + 1], stat[:, kind, 1, b:b + 1])
            if to_xn:
                for b in blocks:
                    op("act", "activation", [b_xt[xb][b], bs[b][2]], [b_xn[b]], out=xn[:, b, :], in_=xt[xb][:, b, :], func=AF.Identity, scale=stat[:, kind, 2, b:b + 1])

        def transposes(c_s, c_sh):
            for j in range(KC):
                bk = next_bank()
                for b in range(NB):
                    op("pe", "transpose", [b_xn[b], b_id], [b_pm[bk]], pm[bk][:, b * 128:(b + 1) * 128], xn[:, b, j * 128:(j + 1) * 128], ident[:], inc=(b == NB - 1))
                op("act", "activation", [b_pm[bk], b_cst], [b_hT[j]], out=hT[:, j, :], in_=pm[bk][:, :], func=AF.Identity, scale=ccol(c_s, j), bias=ccol(c_sh, j))

        def win_piece(fam, g):
            c0 = fam * D + g * 256
            return load_piece("sp", "p (k c) -> p k c", dict(k=8),
                              win_bf.rearrange("(k p) c -> p k c", p=128)[:, :, c0:c0 + 256], [b_s_in[fam]], nelem=2048)

        def proj2(fam, g, consume):
            slot, wv = win_piece(fam, g)
            for l in range(GC):
                bk = next_bank()
                for k in range(KC):
                    op("pe", "matmul", [b_ring[slot], b_hT[k]], [b_pm[bk]], pm[bk][:, :], lhsT=wv[:, k, l * 128:(l + 1) * 128], rhs=hT[:, k, :],
                       start=(k == 0), stop=(k == KC - 1), inc=(k == KC - 1))
                consume(l, bk)

        def lru_part1a(g, rs_=0):
            rxs, b_rxs = rxs_s[rs_], b_rxs_s[rs_]
            j0 = g * GC
            op("dve", "tensor_copy", [b_hrx], [b_rxs[0], b_rxs[1]], rxs[:, :, 0:3], halo_rx[:, j0:j0 + GC, :])

            def c_rx(l, bk):
                op("act", "copy", [b_pm[bk]], [b_rxs[l]], rxs[:, l, 3:T + 3], pm[bk][:, :])
            proj2(FAM_RX, g, c_rx)
            op("dve", "tensor_copy", [b_rxs[0], b_rxs[1]], [b_hrx], halo_rx[:, j0:j0 + GC, :], rxs[:, :, T:T + 3])

        def lru_part1b(g, bs_=0, rs_=0):
            rxs, b_rxs = rxs_s[rs_], b_rxs_s[rs_]
            u, b_u, ubf, b_ubf = u_s[bs_], b_u_s[bs_], ubf_s[bs_], b_ubf_s[bs_]
            j0 = g * GC
            for l in range(GC):
                j = j0 + l
                op("dve", "tensor_scalar", [b_rxs[l], b_vfm], [b_u[l]], out=u[:, l, :], in0=rxs[:, l, 3:T + 3], scalar1=vcol(V_CBW + 24, j), scalar2=vcol(V_CBB, j), op0=ALU.mult, op1=ALU.add)
                for kk in range(1, 4):
                    op("dve", "scalar_tensor_tensor", [b_rxs[l], b_vfm, b_u[l]], [b_u[l]], out=u[:, l, :], in0=rxs[:, l, 3 - kk:T + 3 - kk], scalar=vcol(V_CBW + (3 - kk) * 8, j), in1=u[:, l, :], op0=ALU.mult, op1=ALU.add)
                op("dve", "tensor_copy", [b_u[l]], [b_ubf[l]], ubf[:, l, :], u[:, l, :])

        def lru_part1(g, bs_=0):
            lru_part1a(g, 0)
            lru_part1b(g, bs_, 0)

        def lru_part2(g, first_fix, bs_=0, pool_off=False):
            u, b_u, ubf, b_ubf = u_s[bs_], b_u_s[bs_], ubf_s[bs_], b_ubf_s[bs_]
            j0 = g * GC
            for wsb_, cb_, dst, bdst in ((wga, C_HBA, trt, b_trt), (wgx, C_HBX, TI, b_TI)):
                for l in range(GC):
                    bk = next_bank()
                    for kk in range(2):
                        op("pe", "matmul", [b_wg, b_ubf[kk]], [b_pm[bk]], pm[bk][:, :], lhsT=wsb_[:, g, kk, l * 128:(l + 1) * 128], rhs=ubf[:, kk, :],
                           start=(kk == 0), stop=(kk == 1), inc=(kk == 1))
                    op("act", "activation", [b_pm[bk], b_cst], [bdst[l]], out=dst[:, l, :], in_=pm[bk][:, :], func=AF.Tanh, scale=0.5, bias=ccol(cb_, j0 + l))
            for l in range(GC):
                j = j0 + l
                op("act", "activation", [b_trt[l], b_cst], [b_A[l]], out=Aa[:, l, :], in_=trt[:, l, :], func=AF.Exp, scale=ccol(C_HC, j), bias=ccol(C_HC, j))
                if pool_off:
                    op("pool", "tensor_tensor", [b_A[l]], [b_M[l]], out=Mm[:, l, :], in0=Aa[:, l, :], in1=Aa[:, l, :], op=ALU.mult)
                else:
                    op("act", "activation", [b_trt[l], b_cst], [b_M[l]], out=Mm[:, l, :], in_=trt[:, l, :], func=AF.Exp, scale=ccol(C_CC, j), bias=ccol(C_CC, j))
            for l in range(GC):
                op("act", "activation", [b_M[l]], [b_M[l]], out=Mm[:, l, :], in_=Mm[:, l, :], func=AF.Sqrt, scale=-0.25, bias=0.25)
            for l in range(GC):
                j = j0 + l
                if first_fix == "half":
                    op("dve", "memset", [], [b_M[l]], Mm[:, l, 0:1], 0.5)
                elif first_fix == "flag":
                    op("dve", "tensor_scalar", [b_M[l], b_flg], [b_M[l]], out=Mm[:, l, 0:1], in0=Mm[:, l, 0:1], scalar1=flg[:, 1:2], scalar2=None, op0=ALU.max)
                op("dve", "scalar_tensor_tensor", [b_TI[l], b_u[l]], [b_TI[l]], out=TI[:, l, :], in0=TI[:, l, :], scalar=1.0, in1=u[:, l, :], op0=ALU.add, op1=ALU.mult)
                op("dve", "scalar_tensor_tensor", [b_TI[l], b_M[l]], [b_M[l]], out=Mm[:, l, :], in0=Mm[:, l, :], scalar=0.0, in1=TI[:, l, :], op0=ALU.max, op1=ALU.mult)
                op("dve", "tensor_tensor_scan", [b_A[l], b_M[l], b_hs], [b_hb[l]], out=hb[:, l, :], data0=Aa[:, l, :], data1=Mm[:, l, :], initial=hstate[:, j:j + 1], op0=ALU.mult, op1=ALU.add)
                op("dve", "tensor_copy", [b_hb[l]], [b_hs], hstate[:, j:j + 1], hb[:, l, T - 1:T])

        def p_group(g):
            j0 = g * GC
            op("dve", "tensor_copy", [b_hp], [b_pp[0], b_pp[1]], pp[:, :, 0:2], halo_p[:, j0:j0 + GC, :])

            def c_cx(l, bk):
                op("act", "copy", [b_pm[bk]], [b_cxs[l]], cxs[:, l, :], pm[bk][:, :])
            proj2(FAM_CX, g, c_cx)

            def c_cc(l, bk):
                op("dve", "tensor_tensor", [b_pm[bk], b_cxs[l]], [b_pp[l]], out=pp[:, l, 2:T + 2], in0=pm[bk][:, :], in1=cxs[:, l, :], op=ALU.mult)
            proj2(FAM_CC, g, c_cc)
            op("dve", "tensor_copy", [b_pp[0], b_pp[1]], [b_hp], halo_p[:, j0:j0 + GC, :], pp[:, :, T:T + 2])

        def mixer_group(g, first):
            j0 = g * GC
            lru_part1(g)

            def c_rg(l, bk):
                op("act", "activation", [b_pm[bk]], [b_gl[l]], out=gl[:, l, :], in_=pm[bk][:, :], func=AF.Gelu_apprx_tanh)
            proj2(FAM_RG, g, c_rg)

            def c_gb(l, bk):
                op("act", "activation", [b_pm[bk]], [b_tb[l]], out=tb[:, l, :], in_=pm[bk][:, :], func=AF.Tanh, scale=0.5)
            proj2(FAM_GB, g, c_gb)

            def c_ga(l, bk):
                op("act", "activation", [b_pm[bk]], [b_ta[l]], out=ta[:, l, :], in_=pm[bk][:, :], func=AF.Tanh, scale=0.5)
            proj2(FAM_GA, g, c_ga)
            p_group(g)
            for l in range(GC):
                j = j0 + l
                op("dve", "tensor_scalar", [b_pp[l], b_vfm], [b_ya[l]], out=ya[:, l, :], in0=pp[:, l, 2:T + 2], scalar1=vcol(V_CAW + 16, j), scalar2=None, op0=ALU.mult)
                for kk in range(1, 3):
                    op("dve", "scalar_tensor_tensor", [b_pp[l], b_vfm, b_ya[l]], [b_ya[l]], out=ya[:, l, :], in0=pp[:, l, 2 - kk:T + 2 - kk], scalar=vcol(V_CAW + (2 - kk) * 8, j), in1=ya[:, l, :], op0=ALU.mult, op1=ALU.add)
            lru_part2(g, "flag" if first else None)
            for l in range(GC):
                op("dve", "tensor_tensor", [b_hb[l], b_gl[l]], [b_hb[l]], out=hb[:, l, :], in0=hb[:, l, :], in1=gl[:, l, :], op=ALU.mult)
                op("dve", "scalar_tensor_tensor", [b_tb[l], b_hb[l]], [b_hb[l]], out=hb[:, l, :], in0=tb[:, l, :], scalar=1.0, in1=hb[:, l, :], op0=ALU.add, op1=ALU.mult)

            def c_cb(l, bk):
                j = j0 + l
                op("dve", "tensor_tensor", [b_pm[bk], b_ya[l]], [b_ya[l]], out=ya[:, l, :], in0=pm[bk][:, :], in1=ya[:, l, :], op=ALU.mult)
                op("dve", "scalar_tensor_tensor", [b_ta[l], b_ya[l]], [b_ya[l]], out=ya[:, l, :], in0=ta[:, l, :], scalar=1.0, in1=ya[:, l, :], op0=ALU.add, op1=ALU.mult)
                op("dve", "tensor_tensor", [b_ya[l], b_hb[l]], [b_mT[j]], out=mT[:, j, :], in0=ya[:, l, :], in1=hb[:, l, :], op=ALU.add)
            proj2(FAM_CB, g, c_cb)

        def resid_add(xb, b, hf, bk, bcm):
            sl = (b * 2 + hf) % 2
            op("dve", "tensor_tensor", [b_pm[bk], b_bc], [b_tmpg[sl]], out=tmpg[:, sl, :], in0=pm[bk][:, :], in1=bc[:, bcm, hf * T:(hf + 1) * T], op=ALU.mult)
            op("pool", "tensor_tensor", [b_tmpg[sl], b_xt[xb][b]], [b_xt[xb][b]], out=xt[xb][:, b, hf * T:(hf + 1) * T], in0=tmpg[:, sl, :], in1=xt[xb][:, b, hf * T:(hf + 1) * T], op=ALU.add)

        def out_proj(xb):
            pcs = []
            for hf in range(2):
                pcs.append(load_piece("sp", "p (k c) -> p k c", dict(k=8), wout_bf.rearrange("(k p) c -> p k c", p=128)[:, :, hf * T:(hf + 1) * T], [b_s_out]))
            bks = {}
            JA = KC - GC
            for b in range(NB):
                for hf in range(2):
                    slot, wv = pcs[hf]
                    bk = next_bank()
                    bks[(b, hf)] = bk
                    for j in range(JA):
                        op("pe", "matmul", [b_ring[slot], b_mT[j]], [b_pm[bk]], pm[bk][:, :], lhsT=mT[:, j, b * 128:(b + 1) * 128], rhs=wv[:, j, :],
                           start=(j == 0), stop=False, inc=False)
            for b in range(NB):
                for hf in range(2):
                    slot, wv = pcs[hf]
                    bk = bks[(b, hf)]
                    for j in range(JA, KC):
                        op("pe", "matmul", [b_ring[slot], b_mT[j]], [b_pm[bk]], pm[bk][:, :], lhsT=mT[:, j, b * 128:(b + 1) * 128], rhs=wv[:, j, :],
                           start=False, stop=(j == KC - 1), inc=(j == KC - 1))
                    resid_add(xb, b, hf, bk, 0)
                norm(xb, 1, [b])

        def ffn_up():
            gu = wgu_bf.rearrange("(k p) (two f) -> p k two f", p=128, two=2)
            for q in range(NF // 2):
                slot = st["slot"]
                st["slot"] = (slot + 1) % NSLOT
                wv = ring[:, slot, :].rearrange("p (k two c) -> p k two c", k=8, two=2)
                dma("sp", wv[:, :, 0, :], gu[:, :, 0, q * 256:(q + 1) * 256], ring_sem[slot], [b_s_gu], [b_ring[slot]])
                dma("sp", wv[:, :, 1, :], gu[:, :, 1, q * 256:(q + 1) * 256], ring_sem[slot], [b_s_gu], [b_ring[slot]], nodeps=True)
                for l in range(2):
                    f = q * 2 + l
                    bkg = next_bank()
                    for k in range(KC):
                        op("pe", "matmul", [b_ring[slot], b_hT[k]], [b_pm[bkg]], pm[bkg][:, :], lhsT=wv[:, k, 0, l * 128:(l + 1) * 128], rhs=hT[:, k, :],
                           start=(k == 0), stop=(k == KC - 1), inc=(k == KC - 1))
                    bku = next_bank()
                    for k in range(KC):
                        op("pe", "matmul", [b_ring[slot], b_hT[k]], [b_pm[bku]], pm[bku][:, :], lhsT=wv[:, k, 1, l * 128:(l + 1) * 128], rhs=hT[:, k, :],
                           start=(k == 0), stop=(k == KC - 1), inc=(k == KC - 1))
                    sl = f % 2
                    op("act", "activation", [b_pm[bkg]], [b_sg[sl]], out=sg[:, sl, :], in_=pm[bkg][:, :], func=AF.Silu)
                    op("dve", "tensor_tensor", [b_pm[bku], b_sg[sl]], [b_actT[f]], out=actT[:, f, :], in0=pm[bku][:, :], in1=sg[:, sl, :], op=ALU.mult)

        def ffn_down(xb):
            dn = wdown_bf.rearrange("(f p) c -> p f c", p=128)
            groups = [(0, 8), (8, 8), (16, 6)]
            for hf in range(2):
                bks = [next_bank() for _ in range(NB)]
                for (f0, nf) in groups:
                    slot, wv = load_piece("sp", "p (f c) -> p f c", dict(f=nf), dn[:, f0:f0 + nf, hf * T:(hf + 1) * T], [b_s_dn], nelem=nf * T)
                    for b in range(NB):
                        for fl in range(nf):
                            f = f0 + fl
                            op("pe", "matmul", [b_ring[slot], b_actT[f]], [b_pm[bks[b]]], pm[bks[b]][:, :], lhsT=actT[:, f, b * 128:(b + 1) * 128], rhs=wv[:, fl, :],
                               start=(f == 0), stop=(f == NF - 1), inc=(fl == nf - 1))
                for b in range(NB):
                    resid_add(xb, b, hf, bks[b], 1)

        def final_norm_store(xb, t):
            norm(xb, 2, to_xn=False)
            rs = stat[:, 2, 2, :]
            for b in range(NB):
                op("dve", "scalar_tensor_tensor", [b_xt[xb][b], b_stat[2][b][2], b_bc], [b_xt[xb][b]], out=xt[xb][:, b, :], in0=xt[xb][:, b, :], scalar=rs[:, b:b + 1], in1=bc[:, 2, :], op0=ALU.mult, op1=ALU.mult)
            dma("pool", out[t * T:(t + 1) * T, :].rearrange("(b p) d -> p b d", p=128), xt[xb][:], s_o[xb], b_xt[xb], [])

        seq = [(t, g) for t in range(NT) for g in range(KC // GC)]
        load_x(x_prev, 0, 0)

        def stage1a(i):
            t, g = seq[i]
            if g == 0:
                if t + 1 < NT:
                    load_x(x_prev, t + 1, (t + 1) % 2)
                else:
                    load_x(x_main, 0, (t + 1) % 2)
                norm(t % 2, 0)
                transposes(C_S1, C_SH1)
            issue_casts(2)
            lru_part1a(g, i % 2)

        def stage1b(i):
            t, g = seq[i]
            lru_part1b(g, i % 2, i % 2)

        def stage2(i):
            t, g = seq[i]
            lru_part2(g, "half" if t == 0 else None, i % 2, pool_off=False)
            if t == NT - 1:
                p_group(g)
            ckpt(f"pre{t}.g{g}")

        NS_ = len(seq)
        stage1a(0)
        stage1a(1)
        stage1b(0)
        for i in range(NS_):
            if i + 2 < NS_:
                stage1a(i + 2)
            if i + 1 < NS_:
                stage1b(i + 1)
            stage2(i)
        nx = NT
        issue_casts(1000)
        op("dve", "tensor_scalar", [b_hs, b_flg], [b_hs], out=hstate[:], in0=hstate[:], scalar1=flg[:, 0:1], scalar2=None, op0=ALU.mult)
        op("dve", "tensor_scalar", [b_hrx, b_flg], [b_hrx], out=halo_rx[:], in0=halo_rx[:], scalar1=flg[:, 0:1], scalar2=None, op0=ALU.mult)
        op("dve", "tensor_scalar", [b_hp, b_flg], [b_hp], out=halo_p[:], in0=halo_p[:], scalar1=flg[:, 0:1], scalar2=None, op0=ALU.mult)

        norm(nx % 2, 0)
        transposes(C_S1, C_SH1)
        for t in range(NT):
            xb = nx % 2
            ckpt(f"main{t}.tr")
            for g in range(KC // GC):
                mixer_group(g, t == 0)
                ckpt(f"main{t}.g{g}")
            if t + 1 < NT:
                load_x(x_main, t + 1, (nx + 1) % 2)
            out_proj(xb)
            ckpt(f"main{t}.outproj")
            transposes(C_S2, C_SH2)
            ckpt(f"main{t}.tr2")
            ffn_up()
            ckpt(f"main{t}.ffnup")
            if t + 1 < NT:
                norm((nx + 1) % 2, 0)
            ffn_down(xb)
            ckpt(f"main{t}.ffndown")
            if t + 1 < NT:
                transposes(C_S1, C_SH1)
            final_norm_store(xb, t)
            ckpt(f"main{t}.store")
            nx += 1
        for sc_ in s_o:
            sem_, val_ = sc_
            if val_:
                S.E["pool"]["prog"].append(lambda sem_=sem_, val_=val_: nc.gpsimd.wait_ge(sem_, val_))

        with nc.Block() as block:
            @block.tensor
            def _(e):
                for f in S.E["pe"]["prog"]:
                    f()

            @block.scalar
            def _(e):
                for f in S.E["act"]["prog"]:
                    f()

            @block.vector
            def _(e):
                for f in S.E["dve"]["prog"]:
                    f()

            @block.gpsimd
            def _(e):
                for f in S.E["pool"]["prog"]:
                    f()

            @block.sync
            def _(e):
                for f in S.E["sp"]["prog"]:
                    f()
    return nc


def kernel(x, c, w_ada, b_ada, g_norm_mix, w_in, conv_a_w, conv_b_w, conv_b_bias,
           w_rg_a, b_rg_a, w_rg_x, b_rg_x, lru_lambda, w_out, g_norm_ffn,
           w_gate_up, w_down, g_norm_final, _return_maps=False):
    f32 = np.float32
    x = np.asarray(x, f32)
    ca = lambda a: np.ascontiguousarray(np.asarray(a, f32))
    shared = dict(
        w_ada=ca(w_ada[0]), w_in=ca(w_in[0]), w_rg_a=ca(w_rg_a[0]), w_rg_x=ca(w_rg_x[0]),
        w_out=ca(w_out[0]), w_gu=ca(w_gate_up[0]), w_down=ca(w_down[0]),
    )
    in_maps = []
    for i in range(NCORES):
        b, h = divmod(i, 2)
        rows = [np.asarray(c, f32)[b]] + [np.asarray(b_ada, f32)[0].reshape(6, D)[m] for m in range(6)]
        rows += [np.asarray(g_norm_mix, f32)[0]]
        rows += [np.asarray(conv_a_w, f32)[0][k] for k in range(3)]
        rows += [np.asarray(conv_b_w, f32)[0][k] for k in range(4)]
        rows += [np.asarray(conv_b_bias, f32)[0], np.asarray(b_rg_a, f32)[0], np.asarray(b_rg_x, f32)[0],
                 np.asarray(lru_lambda, f32)[0], np.asarray(g_norm_ffn, f32)[0], np.asarray(g_norm_final, f32)]
        vecs = np.ascontiguousarray(np.stack(rows, 0).reshape(NV * 8, 128))
        flags = np.zeros((128, 2), f32)
        flags[:, 0] = 1.0 if h == 1 else 0.0
        flags[:, 1] = 0.5 if h == 0 else 0.0
        xm = np.ascontiguousarray(x[b, h * TOK:(h + 1) * TOK])
        xp = np.ascontiguousarray(x[b, 0:TOK])
        m = dict(shared)
        m.update(x_main=xm, x_prev=xp, vecs=vecs, flags=flags)
        in_maps.append(m)
    if _return_maps:
        return in_maps
    nc = build_nc()
    res = run_bass_kernel_spmd(nc, in_maps, core_ids=list(range(NCORES)))
    outp = np.empty((4, 2 * TOK, D), f32)
    for i in range(NCORES):
        b, h = divmod(i, 2)
        outp[b, h * TOK:(h + 1) * TOK] = res.results[i]["out"]
    return outp
```

```python
import numpy as np
from contextlib import ExitStack
import concourse.bass as bass
import concourse.mybir as mybir
from concourse.bass_utils import run_bass_kernel_spmd

F32 = mybir.dt.float32
BF16 = mybir.dt.bfloat16
AF = mybir.ActivationFunctionType
ALU = mybir.AluOpType

D = 1024
KC = 8
T = 512
NB = 4
NT = 8
FF = 2816
NF = 22
NCORES = 8
TOK = 4096
EPS = 1e-6
NSLOT = 4
NV = 21


class Buf:
    __slots__ = ("w", "r")

    def __init__(self):
        self.w = None
        self.r = {}


class Sched:
    def __init__(self):
        self.E = {}

    def add_engine(self, name, eng, sem):
        self.E[name] = dict(name=name, eng=eng, sem=sem, count=0, prog=[], waited={})

    def _wait(self, E, tok):
        sem, val, owner = tok
        if owner == "pe" and E["name"] == "pe":
            return
        key = id(sem)
        if E["waited"].get(key, 0) >= val:
            return
        E["waited"][key] = val
        eng = E["eng"]
        E["prog"].append(lambda: eng.wait_ge(sem, val))

    def _deps(self, E, reads, writes):
        for b in reads:
            if b.w is not None:
                self._wait(E, b.w)
        for b in writes:
            if b.w is not None:
                self._wait(E, b.w)
            for t in b.r.values():
                self._wait(E, t)

    @staticmethod
    def _upd(tok, reads, writes):
        for b in writes:
            b.w = tok
            b.r = {}
        for b in reads:
            if b.w is tok:
                continue
            b.r[id(tok[0])] = tok

    def op(self, ename, fn, reads=(), writes=(), inc=True):
        E = self.E[ename]
        self._deps(E, reads, writes)
        seq = E["count"] + 1
        sem = E["sem"]
        if inc:
            E["count"] = seq
            E["prog"].append(lambda: fn().then_inc(sem, 1))
        else:
            E["prog"].append(fn)
        self._upd((sem, seq, ename), reads, writes)

    def dma(self, qname, fn, semc, reads=(), writes=(), nodeps=False):
        E = self.E[qname]
        if not nodeps:
            self._deps(E, reads, writes)
        semc[1] += 16
        sem = semc[0]
        E["prog"].append(lambda: fn().then_inc(sem, 16))
        self._upd((sem, semc[1], "dma"), reads, writes)


class _Stop(Exception):
    pass


def build_nc(limit=None, dbg=None):
    nc = bass.Bass("TRN2", target_bir_lowering=False)
    _ck = [0]

    def ckpt(name):
        _ck[0] += 1
        if limit is not None and _ck[0] == limit:
            print("STOP at ckpt", _ck[0], name)
            _ck.append("stopped")

    def stopped():
        return len(_ck) > 1

    def din(name, shape, dt=F32):
        return nc.dram_tensor(name, shape, dt, kind="ExternalInput").ap()

    x_main = din("x_main", [TOK, D])
    x_prev = din("x_prev", [TOK, D])
    vecs = din("vecs", [NV * 8, 128])
    flags = din("flags", [128, 2])
    w_ada = din("w_ada", [D, 6 * D])
    w_in = din("w_in", [D, 7 * D])
    w_rg_a = din("w_rg_a", [4, 256, 256])
    w_rg_x = din("w_rg_x", [4, 256, 256])
    w_out = din("w_out", [D, D])
    w_gu = din("w_gu", [D, 2 * FF])
    w_down = din("w_down", [FF, D])
    out = nc.dram_tensor("out", [TOK, D], F32, kind="ExternalOutput").ap()

    def dscr(name, shape):
        return nc.dram_tensor(name, shape, BF16, kind="Internal").ap()

    win_bf = dscr("win_bf", [D, 7 * D])
    wgu_bf = dscr("wgu_bf", [D, 2 * FF])
    wdown_bf = dscr("wdown_bf", [FF, D])
    wout_bf = dscr("wout_bf", [D, D])
    wrga_bf = dscr("wrga_bf", [4, 256, 256])
    wrgx_bf = dscr("wrgx_bf", [4, 256, 256])

    with ExitStack() as es:
        def sb(name, shape, dt=F32):
            return es.enter_context(nc.sbuf_tensor(name, shape, dt))

        def ps(name, shape, dt=F32):
            return es.enter_context(nc.psum_tensor(name, shape, dt))

        def semc(name):
            return [es.enter_context(nc.semaphore(name)), 0]

        S = Sched()
        ENG = dict(pe=nc.tensor, act=nc.scalar, dve=nc.vector, pool=nc.gpsimd, sp=nc.sync)
        for nm, e in ENG.items():
            S.add_engine(nm, e, es.enter_context(nc.semaphore("s_" + nm)))

        def op(e, meth, reads, writes, *args, inc=True, **kw):
            if stopped():
                return
            eng = ENG[e]
            S.op(e, lambda: getattr(eng, meth)(*args, **kw), reads, writes, inc)

        def dma(q, out_ap, in_ap, sc, reads, writes, nodeps=False):
            if stopped():
                return
            eng = ENG[q]
            S.dma(q, lambda: eng.dma_start(out=out_ap, in_=in_ap), sc, reads, writes, nodeps)

        xt = [sb(f"xt{i}", [128, NB, D]) for i in range(2)]
        b_xt = [[Buf() for _ in range(NB)] for _ in range(2)]
        xn = sb("xn", [128, NB, D]); b_xn = [Buf() for _ in range(NB)]
        junk = sb("junk", [128, D], BF16); b_junk = Buf()
        hT = sb("hT", [128, KC, T], BF16); b_hT = [Buf() for _ in range(KC)]
        ring = sb("ring", [128, NSLOT, 4096], BF16); b_ring = [Buf() for _ in range(NSLOT)]
        ring_sem = [semc(f"rs{i}") for i in range(NSLOT)]
        ring_sem_sw = [semc(f"rw{i}") for i in range(NSLOT)]
        wga = sb("wga", [128, 4, 2, 256], BF16)
        wgx = sb("wgx", [128, 4, 2, 256], BF16)
        b_wg = Buf()
        vrow0 = sb("vrow0", [128, 128]); vrow1 = sb("vrow1", [NV * 8 - 128, 128]); b_vrow = Buf()
        vfm = sb("vfm", [128, NV * 8]); b_vfm = Buf()
        ident = sb("ident", [128, 128]); identb = sb("identb", [128, 128], BF16); b_id = Buf()
        ones = sb("ones", [128, 128]); b_ones = Buf()
        cst = sb("cst", [128, 16 * 8]); b_cst = Buf()
        cactb = sb("cactb", [128, 8], BF16)
        modv = sb("modv", [128, 48]); b_mod = Buf()
        flg = sb("flg", [128, 2]); b_flg = Buf()
        bc = sb("bc", [128, 3, D]); b_bc = Buf()
        diag = sb("diag", [128, 2, 128]); b_diag = [Buf(), Buf()]
        stat = sb("stat", [128, 3, 3, NB]); b_stat = [[[Buf() for _ in range(3)] for _ in range(NB)] for _ in range(3)]
        cm05 = sb("cm05", [128, NB]); b_cm05 = Buf()
        hstate = sb("hstate", [128, KC]); b_hs = Buf()
        halo_rx = sb("halo_rx", [128, KC, 3]); b_hrx = Buf()
        halo_p = sb("halo_p", [128, KC, 2]); b_hp = Buf()
        GC = 2
        rxs_s = [sb(f"rxs{i}", [128, GC, T + 3]) for i in range(2)]; b_rxs_s = [[Buf() for _ in range(GC)] for _ in range(2)]
        u_s = [sb(f"u{i}", [128, GC, T]) for i in range(2)]; b_u_s = [[Buf() for _ in range(GC)] for _ in range(2)]
        ubf_s = [sb(f"ubf{i}", [128, GC, T], BF16) for i in range(2)]; b_ubf_s = [[Buf() for _ in range(GC)] for _ in range(2)]
        Aa = sb("Aa", [128, GC, T]); b_A = [Buf() for _ in range(GC)]
        Mm = sb("Mm", [128, GC, T]); b_M = [Buf() for _ in range(GC)]
        TI = sb("TI", [128, GC, T]); b_TI = [Buf() for _ in range(GC)]
        trt = sb("trt", [128, GC, T]); b_trt = [Buf() for _ in range(GC)]
        hb = sb("hb", [128, GC, T]); b_hb = [Buf() for _ in range(GC)]
        gl = sb("gl", [128, GC, T], BF16); b_gl = [Buf() for _ in range(GC)]
        tb = sb("tb", [128, GC, T]); b_tb = [Buf() for _ in range(GC)]
        ta = sb("ta", [128, GC, T]); b_ta = [Buf() for _ in range(GC)]
        cxs = sb("cxs", [128, GC, T]); b_cxs = [Buf() for _ in range(GC)]
        pp = sb("pp", [128, GC, T + 2]); b_pp = [Buf() for _ in range(GC)]
        ya = sb("ya", [128, GC, T]); b_ya = [Buf() for _ in range(GC)]
        actT = sb("actT", [128, NF, T], BF16); b_actT = [Buf() for _ in range(NF)]
        mT = actT; b_mT = b_actT
        sg = sb("sg", [128, 2, T]); b_sg = [Buf(), Buf()]
        tmpg = sb("tmpg", [128, 2, T]); b_tmpg = [Buf(), Buf()]
        NPM = 8
        pm = [ps(f"pm{i}", [128, T]) for i in range(NPM)]; b_pm = [Buf() for _ in range(NPM)]
        st = dict(bank=0, slot=0, tp=0)

        def next_bank():
            st["bank"] = (st["bank"] + 1) % NPM
            return st["bank"]

        C_S1, C_SH1, C_S2, C_SH2, C_GT1H, C_GT2, C_HC, C_CC, C_HBA, C_HBX, C_TMP, C_TMP2 = [i * 8 for i in range(12)]
        V_C, V_BADA, V_GMIX, V_CAW, V_CBW, V_CBB, V_BRA, V_BRX, V_LAM, V_GFFN, V_GFIN = 0, 8, 56, 64, 88, 120, 128, 136, 144, 152, 160

        def vcol(base, j):
            return vfm[:, base + j:base + j + 1]

        def ccol(base, j):
            return cst[:, base + j:base + j + 1]

        b_s_g = Buf(); s_g = semc("cg")
        dma("pool", wrga_bf, w_rg_a, s_g, [], [b_s_g])
        dma("pool", wrgx_bf, w_rg_x, s_g, [], [b_s_g])
        s_small = semc("csm")
        vv = vecs
        dma("sp", vrow0[:], vv[0:128, :], s_small, [], [b_vrow])
        dma("sp", vrow1[:], vv[128:NV * 8, :], s_small, [], [b_vrow])
        s_flg = semc("cflg")
        dma("sp", flg[:], flags, s_flg, [], [b_flg])
        ada_slots = []

        def ring_view(slot, pat, **kw):
            return ring[:, slot, :].rearrange(pat, **kw)

        def load_piece(q, view_pat, view_kw, src, reads, nelem=4096):
            slot = st["slot"]
            st["slot"] = (slot + 1) % NSLOT
            dst = ring[:, slot, 0:nelem].rearrange(view_pat, **view_kw)
            dma(q, dst, src, (ring_sem_sw if q == "pool" else ring_sem)[slot], reads, [b_ring[slot]])
            return slot, dst

        op("pool", "memset", [], [b_id], ident[:], 0.0)
        op("pool", "affine_select", [b_id], [b_id], out=ident[:], in_=ident[:], compare_op=ALU.not_equal,
           fill=1.0, base=0, pattern=[[-1, 128]], channel_multiplier=1)
        op("dve", "tensor_copy", [b_id], [b_id], identb[:], ident[:])
        op("pool", "memset", [], [b_ones], ones[:], 1.0)
        op("pool", "memset", [], [b_cm05], cm05[:], -0.5)
        op("pool", "memset", [], [b_hs], hstate[:], 0.0)
        op("pool", "memset", [], [b_hrx], halo_rx[:], 0.0)
        op("pool", "memset", [], [b_hp], halo_p[:], 0.0)

        ckpt("consts")
        bk = next_bank()
        op("pe", "transpose", [b_vrow, b_id], [b_pm[bk]], pm[bk][:, 0:128], vrow0[:], ident[:])
        op("pe", "transpose", [b_vrow, b_id], [b_pm[bk]], pm[bk][:, 128:NV * 8], vrow1[:], ident[0:NV * 8 - 128, 0:NV * 8 - 128])
        op("dve", "tensor_copy", [b_pm[bk]], [b_vfm], vfm[:], pm[bk][:, 0:NV * 8])
        op("act", "activation", [b_vfm], [b_cst], out=cst[:, C_TMP:C_TMP + 8], in_=vfm[:, V_C:V_C + 8], func=AF.Silu)
        op("dve", "tensor_copy", [b_cst], [b_cst], cactb[:], cst[:, C_TMP:C_TMP + 8])
        ckpt("vfm")
        bkm = next_bank()
        for q in range(12):
            slot, dst = load_piece("pool", "p (k c) -> p k c", dict(k=8),
                                   w_ada.rearrange("(k p) c -> p k c", p=128)[:, :, q * 512:(q + 1) * 512], [])
            for cc in range(4):
                oc = q * 4 + cc
                for k in range(8):
                    op("pe", "matmul", [b_ring[slot], b_cst], [b_pm[bkm]], pm[bkm][:, oc:oc + 1],
                       lhsT=dst[:, k, cc * 128:(cc + 1) * 128], rhs=cactb[:, k:k + 1],
                       start=(k == 0), stop=(k == 7), inc=(k == 7))
        op("dve", "tensor_tensor", [b_pm[bkm], b_vfm], [b_mod], out=modv[:], in0=pm[bkm][:, 0:48], in1=vfm[:, V_BADA:V_BADA + 48], op=ALU.add)
        ckpt("mod")
        R = [b_mod, b_vfm, b_cst]
        W = [b_cst]
        op("dve", "scalar_tensor_tensor", R, W, out=cst[:, C_S1:C_S1 + 8], in0=modv[:, 8:16], scalar=1.0, in1=vfm[:, V_GMIX:V_GMIX + 8], op0=ALU.add, op1=ALU.mult)
        op("dve", "tensor_copy", R, W, cst[:, C_SH1:C_SH1 + 8], modv[:, 0:8])
        op("dve", "scalar_tensor_tensor", R, W, out=cst[:, C_S2:C_S2 + 8], in0=modv[:, 32:40], scalar=1.0, in1=vfm[:, V_GFFN:V_GFFN + 8], op0=ALU.add, op1=ALU.mult)
        op("dve", "tensor_copy", R, W, cst[:, C_SH2:C_SH2 + 8], modv[:, 24:32])
        op("dve", "tensor_scalar", R, W, out=cst[:, C_GT1H:C_GT1H + 8], in0=modv[:, 16:24], scalar1=0.5, scalar2=None, op0=ALU.mult)
        op("dve", "tensor_copy", R, W, cst[:, C_GT2:C_GT2 + 8], modv[:, 40:48])
        op("act", "activation", R, W, out=cst[:, C_TMP:C_TMP + 8], in_=vfm[:, V_LAM:V_LAM + 8], func=AF.Exp, scale=-1.0)
        op("act", "activation", R, W, out=cst[:, C_TMP2:C_TMP2 + 8], in_=cst[:, C_TMP:C_TMP + 8], func=AF.Ln, bias=1.0)
        op("dve", "tensor_scalar", R, W, out=cst[:, C_CC:C_CC + 8], in0=cst[:, C_TMP2:C_TMP2 + 8], scalar1=-8.0, scalar2=None, op0=ALU.mult)
        op("dve", "tensor_scalar", R, W, out=cst[:, C_HC:C_HC + 8], in0=cst[:, C_TMP2:C_TMP2 + 8], scalar1=-4.0, scalar2=None, op0=ALU.mult)
        op("dve", "tensor_scalar", R, W, out=cst[:, C_HBA:C_HBA + 8], in0=vfm[:, V_BRA:V_BRA + 8], scalar1=0.5, scalar2=None, op0=ALU.mult)
        op("dve", "tensor_scalar", R, W, out=cst[:, C_HBX:C_HBX + 8], in0=vfm[:, V_BRX:V_BRX + 8], scalar1=0.5, scalar2=None, op0=ALU.mult)
        ckpt("derived")
        srcs = [(cst, C_GT1H), (cst, C_GT2), (vfm, V_GFIN)]
        n_d = 0
        for m, (tt, base) in enumerate(srcs):
            for j in range(8):
                dsl = n_d % 2
                n_d += 1
                op("dve", "tensor_scalar", [b_id, b_cst, b_vfm], [b_diag[dsl]], out=diag[:, dsl, :], in0=ident[:], scalar1=tt[:, base + j:base + j + 1], scalar2=None, op0=ALU.mult)
                bk = next_bank()
                op("pe", "matmul", [b_ones, b_diag[dsl]], [b_pm[bk]], pm[bk][:, 0:128], lhsT=ones[:], rhs=diag[:, dsl, :], start=True, stop=True)
                op("act", "copy", [b_pm[bk]], [b_bc], bc[:, m, j * 128:(j + 1) * 128], pm[bk][:, 0:128])

        ckpt("bc")
        s_wg = semc("cwg")
        dma("sp", wga[:], wrga_bf.rearrange("h (kk p) c -> p h kk c", p=128), s_wg, [b_s_g], [b_wg])
        dma("sp", wgx[:], wrgx_bf.rearrange("h (kk p) c -> p h kk c", p=128), s_wg, [b_s_g], [b_wg])

        b_s_in = [Buf() for _ in range(7)]
        s_in = [semc(f"cin{f}") for f in range(7)]
        FAM_CB, FAM_CC, FAM_CX, FAM_RX, FAM_RG, FAM_GA, FAM_GB = range(7)

        b_s_out = Buf(); s_out = semc("cout")
        b_s_gu = Buf(); s_gu = semc("cgu")
        b_s_dn = Buf(); s_dn = semc("cdn")
        cast_jobs = []

        def cast_fam(f):
            for r in range(4):
                cast_jobs.append((win_bf[r * 256:(r + 1) * 256, f * D:(f + 1) * D], w_in[r * 256:(r + 1) * 256, f * D:(f + 1) * D], s_in[f], b_s_in[f]))

        for f in (FAM_RX, FAM_CX, FAM_CC, FAM_RG, FAM_GB, FAM_GA, FAM_CB):
            cast_fam(f)
        for r in range(4):
            cast_jobs.append((wout_bf[r * 256:(r + 1) * 256, :], w_out[r * 256:(r + 1) * 256, :], s_out, b_s_out))
        for r in range(4):
            for cq in range(4):
                cast_jobs.append((wgu_bf[r * 256:(r + 1) * 256, cq * 1408:(cq + 1) * 1408], w_gu[r * 256:(r + 1) * 256, cq * 1408:(cq + 1) * 1408], s_gu, b_s_gu))
        for r in range(11):
            cast_jobs.append((wdown_bf[r * 256:(r + 1) * 256, :], w_down[r * 256:(r + 1) * 256, :], s_dn, b_s_dn))

        def issue_casts(n):
            for _ in range(n):
                if cast_jobs:
                    o_, i_, sc_, b_ = cast_jobs.pop(0)
                    dma("pool", o_, i_, sc_, [], [b_], nodeps=True)

        issue_casts(4)
        ckpt("casts")
        s_x = [semc("sx0"), semc("sx1")]
        s_o = [semc("so0"), semc("so1")]

        def load_x(src, t, xb):
            dma("sp", xt[xb][:], src[t * T:(t + 1) * T, :].rearrange("(b p) d -> p b d", p=128), s_x[xb], [], b_xt[xb])

        def norm(xb, kind, blocks=(0, 1, 2, 3), to_xn=True):
            bs = b_stat[kind]
            for b in blocks:
                op("act", "activation", [b_xt[xb][b]], [b_junk, bs[b][0]], out=junk[:], in_=xt[xb][:, b, :], func=AF.Square, accum_out=stat[:, kind, 0, b:b + 1])
            for b in blocks:
                op("dve", "tensor_scalar", [bs[b][0]], [bs[b][1]], out=stat[:, kind, 1, b:b + 1], in0=stat[:, kind, 0, b:b + 1], scalar1=1.0 / D, scalar2=EPS, op0=ALU.mult, op1=ALU.add)
            for b in blocks:
                op("act", "activation", [bs[b][1]], [bs[b][1]], out=stat[:, kind, 1, b:b + 1], in_=stat[:, kind, 1, b:b + 1], func=AF.Sqrt)
            for b in blocks:
                op("dve", "reciprocal", [bs[b][1]], [bs[b][2]], stat[:, kind, 2, b:b + 1], stat[:, kind, 1, b:b + 1])
            if to_xn:
                for b in blocks:
                    op("act", "activation", [b_xt[xb][b], bs[b][2]], [b_xn[b]], out=xn[:, b, :], in_=xt[xb][:, b, :], func=AF.Identity, scale=stat[:, kind, 2, b:b + 1])

        def transposes(c_s, c_sh):
            for j in range(KC):
                bk = next_bank()
                for b in range(NB):
                    op("pe", "transpose", [b_xn[b], b_id], [b_pm[bk]], pm[bk][:, b * 128:(b + 1) * 128], xn[:, b, j * 128:(j + 1) * 128], ident[:], inc=(b == NB - 1))
                op("act", "activation", [b_pm[bk], b_cst], [b_hT[j]], out=hT[:, j, :], in_=pm[bk][:, :], func=AF.Identity, scale=ccol(c_s, j), bias=ccol(c_sh, j))

        def win_piece(fam, g):
            c0 = fam * D + g * 256
            return load_piece("sp", "p (k c) -> p k c", dict(k=8),
                              win_bf.rearrange("(k p) c -> p k c", p=128)[:, :, c0:c0 + 256], [b_s_in[fam]], nelem=2048)

        def proj2(fam, g, consume):
            slot, wv = win_piece(fam, g)
            for l in range(GC):
                bk = next_bank()
                for k in range(KC):
                    op("pe", "matmul", [b_ring[slot], b_hT[k]], [b_pm[bk]], pm[bk][:, :], lhsT=wv[:, k, l * 128:(l + 1) * 128], rhs=hT[:, k, :],
                       start=(k == 0), stop=(k == KC - 1), inc=(k == KC - 1))
                consume(l, bk)

        def lru_part1a(g, rs_=0):
            rxs, b_rxs = rxs_s[rs_], b_rxs_s[rs_]
            j0 = g * GC
            op("dve", "tensor_copy", [b_hrx], [b_rxs[0], b_rxs[1]], rxs[:, :, 0:3], halo_rx[:, j0:j0 + GC, :])

            def c_rx(l, bk):
                op("act", "copy", [b_pm[bk]], [b_rxs[l]], rxs[:, l, 3:T + 3], pm[bk][:, :])
            proj2(FAM_RX, g, c_rx)

        def lru_part1b(g, bs_=0, rs_=0):
            rxs, b_rxs = rxs_s[rs_], b_rxs_s[rs_]
            u, b_u, ubf, b_ubf = u_s[bs_], b_u_s[bs_], ubf_s[bs_], b_ubf_s[bs_]
            j0 = g * GC
            op("dve", "tensor_copy", [b_rxs[0], b_rxs[1]], [b_hrx], halo_rx[:, j0:j0 + GC, :], rxs[:, :, T:T + 3])
            for l in range(GC):
                j = j0 + l
                op("dve", "tensor_scalar", [b_rxs[l], b_vfm], [b_u[l]], out=u[:, l, :], in0=rxs[:, l, 3:T + 3], scalar1=vcol(V_CBW + 24, j), scalar2=vcol(V_CBB, j), op0=ALU.mult, op1=ALU.add)
                for kk in range(1, 4):
                    op("dve", "scalar_tensor_tensor", [b_rxs[l], b_vfm, b_u[l]], [b_u[l]], out=u[:, l, :], in0=rxs[:, l, 3 - kk:T + 3 - kk], scalar=vcol(V_CBW + (3 - kk) * 8, j), in1=u[:, l, :], op0=ALU.mult, op1=ALU.add)
                op("dve", "tensor_copy", [b_u[l]], [b_ubf[l]], ubf[:, l, :], u[:, l, :])

        def lru_part1(g, bs_=0):
            lru_part1a(g, 0)
            lru_part1b(g, bs_, 0)

        def lru_part2(g, first_fix, bs_=0, pool_off=False):
            u, b_u, ubf, b_ubf = u_s[bs_], b_u_s[bs_], ubf_s[bs_], b_ubf_s[bs_]
            j0 = g * GC
            for wsb_, cb_, dst, bdst in ((wga, C_HBA, trt, b_trt), (wgx, C_HBX, TI, b_TI)):
                for l in range(GC):
                    bk = next_bank()
                    for kk in range(2):
                        op("pe", "matmul", [b_wg, b_ubf[kk]], [b_pm[bk]], pm[bk][:, :], lhsT=wsb_[:, g, kk, l * 128:(l + 1) * 128], rhs=ubf[:, kk, :],
                           start=(kk == 0), stop=(kk == 1), inc=(kk == 1))
                    op("act", "activation", [b_pm[bk], b_cst], [bdst[l]], out=dst[:, l, :], in_=pm[bk][:, :], func=AF.Tanh, scale=0.5, bias=ccol(cb_, j0 + l))
            for l in range(GC):
                j = j0 + l
                op("act", "activation", [b_trt[l], b_cst], [b_A[l]], out=Aa[:, l, :], in_=trt[:, l, :], func=AF.Exp, scale=ccol(C_HC, j), bias=ccol(C_HC, j))
                if pool_off:
                    op("pool", "tensor_tensor", [b_A[l]], [b_M[l]], out=Mm[:, l, :], in0=Aa[:, l, :], in1=Aa[:, l, :], op=ALU.mult)
                else:
                    op("act", "activation", [b_trt[l], b_cst], [b_M[l]], out=Mm[:, l, :], in_=trt[:, l, :], func=AF.Exp, scale=ccol(C_CC, j), bias=ccol(C_CC, j))
            for l in range(GC):
                op("act", "activation", [b_M[l]], [b_M[l]], out=Mm[:, l, :], in_=Mm[:, l, :], func=AF.Sqrt, scale=-0.25, bias=0.25)
            for l in range(GC):
                j = j0 + l
                if first_fix == "half":
                    op("dve", "memset", [], [b_M[l]], Mm[:, l, 0:1], 0.5)
                elif first_fix == "flag":
                    op("dve", "tensor_scalar", [b_M[l], b_flg], [b_M[l]], out=Mm[:, l, 0:1], in0=Mm[:, l, 0:1], scalar1=flg[:, 1:2], scalar2=None, op0=ALU.max)
                op("dve", "scalar_tensor_tensor", [b_TI[l], b_u[l]], [b_TI[l]], out=TI[:, l, :], in0=TI[:, l, :], scalar=1.0, in1=u[:, l, :], op0=ALU.add, op1=ALU.mult)
                op("dve", "scalar_tensor_tensor", [b_TI[l], b_M[l]], [b_M[l]], out=Mm[:, l, :], in0=Mm[:, l, :], scalar=0.0, in1=TI[:, l, :], op0=ALU.max, op1=ALU.mult)
                op("dve", "tensor_tensor_scan", [b_A[l], b_M[l], b_hs], [b_hb[l]], out=hb[:, l, :], data0=Aa[:, l, :], data1=Mm[:, l, :], initial=hstate[:, j:j + 1], op0=ALU.mult, op1=ALU.add)
                op("dve", "tensor_copy", [b_hb[l]], [b_hs], hstate[:, j:j + 1], hb[:, l, T - 1:T])

        def p_group(g):
            j0 = g * GC
            op("dve", "tensor_copy", [b_hp], [b_pp[0], b_pp[1]], pp[:, :, 0:2], halo_p[:, j0:j0 + GC, :])

            def c_cx(l, bk):
                op("act", "copy", [b_pm[bk]], [b_cxs[l]], cxs[:, l, :], pm[bk][:, :])
            proj2(FAM_CX, g, c_cx)

            def c_cc(l, bk):
                op("dve", "tensor_tensor", [b_pm[bk], b_cxs[l]], [b_pp[l]], out=pp[:, l, 2:T + 2], in0=pm[bk][:, :], in1=cxs[:, l, :], op=ALU.mult)
            proj2(FAM_CC, g, c_cc)
            op("dve", "tensor_copy", [b_pp[0], b_pp[1]], [b_hp], halo_p[:, j0:j0 + GC, :], pp[:, :, T:T + 2])

        def mixer_group(g, first):
            j0 = g * GC
            lru_part1(g)

            def c_rg(l, bk):
                op("act", "activation", [b_pm[bk]], [b_gl[l]], out=gl[:, l, :], in_=pm[bk][:, :], func=AF.Gelu_apprx_tanh)
            proj2(FAM_RG, g, c_rg)

            def c_gb(l, bk):
                op("act", "activation", [b_pm[bk]], [b_tb[l]], out=tb[:, l, :], in_=pm[bk][:, :], func=AF.Tanh, scale=0.5)
            proj2(FAM_GB, g, c_gb)

            def c_ga(l, bk):
                op("act", "activation", [b_pm[bk]], [b_ta[l]], out=ta[:, l, :], in_=pm[bk][:, :], func=AF.Tanh, scale=0.5)
            proj2(FAM_GA, g, c_ga)
            p_group(g)
            for l in range(GC):
                j = j0 + l
                op("dve", "tensor_scalar", [b_pp[l], b_vfm], [b_ya[l]], out=ya[:, l, :], in0=pp[:, l, 2:T + 2], scalar1=vcol(V_CAW + 16, j), scalar2=None, op0=ALU.mult)
                for kk in range(1, 3):
                    op("dve", "scalar_tensor_tensor", [b_pp[l], b_vfm, b_ya[l]], [b_ya[l]], out=ya[:, l, :], in0=pp[:, l, 2 - kk:T + 2 - kk], scalar=vcol(V_CAW + (2 - kk) * 8, j), in1=ya[:, l, :], op0=ALU.mult, op1=ALU.add)
            lru_part2(g, "flag" if first else None)
            for l in range(GC):
                op("dve", "tensor_tensor", [b_hb[l], b_gl[l]], [b_hb[l]], out=hb[:, l, :], in0=hb[:, l, :], in1=gl[:, l, :], op=ALU.mult)
                op("dve", "scalar_tensor_tensor", [b_tb[l], b_hb[l]], [b_hb[l]], out=hb[:, l, :], in0=tb[:, l, :], scalar=1.0, in1=hb[:, l, :], op0=ALU.add, op1=ALU.mult)

            def c_cb(l, bk):
                j = j0 + l
                op("dve", "tensor_tensor", [b_pm[bk], b_ya[l]], [b_ya[l]], out=ya[:, l, :], in0=pm[bk][:, :], in1=ya[:, l, :], op=ALU.mult)
                op("dve", "scalar_tensor_tensor", [b_ta[l], b_ya[l]], [b_ya[l]], out=ya[:, l, :], in0=ta[:, l, :], scalar=1.0, in1=ya[:, l, :], op0=ALU.add, op1=ALU.mult)
                op("dve", "tensor_tensor", [b_ya[l], b_hb[l]], [b_mT[j]], out=mT[:, j, :], in0=ya[:, l, :], in1=hb[:, l, :], op=ALU.add)
            proj2(FAM_CB, g, c_cb)

        def resid_add(xb, b, hf, bk, bcm):
            sl = (b * 2 + hf) % 2
            op("dve", "tensor_tensor", [b_pm[bk], b_bc], [b_tmpg[sl]], out=tmpg[:, sl, :], in0=pm[bk][:, :], in1=bc[:, bcm, hf * T:(hf + 1) * T], op=ALU.mult)
            op("pool", "tensor_tensor", [b_tmpg[sl], b_xt[xb][b]], [b_xt[xb][b]], out=xt[xb][:, b, hf * T:(hf + 1) * T], in0=tmpg[:, sl, :], in1=xt[xb][:, b, hf * T:(hf + 1) * T], op=ALU.add)

        def out_proj(xb):
            pcs = []
            for hf in range(2):
                pcs.append(load_piece("sp", "p (k c) -> p k c", dict(k=8), wout_bf.rearrange("(k p) c -> p k c", p=128)[:, :, hf * T:(hf + 1) * T], [b_s_out]))
            bks = {}
            JA = KC - GC
            for b in range(NB):
                for hf in range(2):
                    slot, wv = pcs[hf]
                    bk = next_bank()
                    bks[(b, hf)] = bk
                    for j in range(JA):
                        op("pe", "matmul", [b_ring[slot], b_mT[j]], [b_pm[bk]], pm[bk][:, :], lhsT=mT[:, j, b * 128:(b + 1) * 128], rhs=wv[:, j, :],
                           start=(j == 0), stop=False, inc=False)
            for b in range(NB):
                for hf in range(2):
                    slot, wv = pcs[hf]
                    bk = bks[(b, hf)]
                    for j in range(JA, KC):
                        op("pe", "matmul", [b_ring[slot], b_mT[j]], [b_pm[bk]], pm[bk][:, :], lhsT=mT[:, j, b * 128:(b + 1) * 128], rhs=wv[:, j, :],
                           start=False, stop=(j == KC - 1), inc=(j == KC - 1))
                    resid_add(xb, b, hf, bk, 0)
                norm(xb, 1, [b])

        def ffn_up():
            gu = wgu_bf.rearrange("(k p) (two f) -> p k two f", p=128, two=2)
            for q in range(NF // 2):
                slot = st["slot"]
                st["slot"] = (slot + 1) % NSLOT
                wv = ring[:, slot, :].rearrange("p (k two c) -> p k two c", k=8, two=2)
                dma("sp", wv[:, :, 0, :], gu[:, :, 0, q * 256:(q + 1) * 256], ring_sem[slot], [b_s_gu], [b_ring[slot]])
                dma("sp", wv[:, :, 1, :], gu[:, :, 1, q * 256:(q + 1) * 256], ring_sem[slot], [b_s_gu], [b_ring[slot]], nodeps=True)
                for l in range(2):
                    f = q * 2 + l
                    bkg = next_bank()
                    for k in range(KC):
                        op("pe", "matmul", [b_ring[slot], b_hT[k]], [b_pm[bkg]], pm[bkg][:, :], lhsT=wv[:, k, 0, l * 128:(l + 1) * 128], rhs=hT[:, k, :],
                           start=(k == 0), stop=(k == KC - 1), inc=(k == KC - 1))
                    bku = next_bank()
                    for k in range(KC):
                        op("pe", "matmul", [b_ring[slot], b_hT[k]], [b_pm[bku]], pm[bku][:, :], lhsT=wv[:, k, 1, l * 128:(l + 1) * 128], rhs=hT[:, k, :],
                           start=(k == 0), stop=(k == KC - 1), inc=(k == KC - 1))
                    sl = f % 2
                    op("act", "activation", [b_pm[bkg]], [b_sg[sl]], out=sg[:, sl, :], in_=pm[bkg][:, :], func=AF.Silu)
                    op("dve", "tensor_tensor", [b_pm[bku], b_sg[sl]], [b_actT[f]], out=actT[:, f, :], in0=pm[bku][:, :], in1=sg[:, sl, :], op=ALU.mult)

        def ffn_down(xb):
            dn = wdown_bf.rearrange("(f p) c -> p f c", p=128)
            groups = [(0, 8), (8, 8), (16, 6)]
            for hf in range(2):
                bks = [next_bank() for _ in range(NB)]
                for (f0, nf) in groups:
                    slot, wv = load_piece("sp", "p (f c) -> p f c", dict(f=nf), dn[:, f0:f0 + nf, hf * T:(hf + 1) * T], [b_s_dn], nelem=nf * T)
                    for b in range(NB):
                        for fl in range(nf):
                            f = f0 + fl
                            op("pe", "matmul", [b_ring[slot], b_actT[f]], [b_pm[bks[b]]], pm[bks[b]][:, :], lhsT=actT[:, f, b * 128:(b + 1) * 128], rhs=wv[:, fl, :],
                               start=(f == 0), stop=(f == NF - 1), inc=(fl == nf - 1))
                for b in range(NB):
                    resid_add(xb, b, hf, bks[b], 1)

        def final_norm_store(xb, t):
            norm(xb, 2, to_xn=False)
            rs = stat[:, 2, 2, :]
            for b in range(NB):
                op("dve", "scalar_tensor_tensor", [b_xt[xb][b], b_stat[2][b][2], b_bc], [b_xt[xb][b]], out=xt[xb][:, b, :], in0=xt[xb][:, b, :], scalar=rs[:, b:b + 1], in1=bc[:, 2, :], op0=ALU.mult, op1=ALU.mult)
            dma("pool", out[t * T:(t + 1) * T, :].rearrange("(b p) d -> p b d", p=128), xt[xb][:], s_o[xb], b_xt[xb], [])

        seq = [(t, g) for t in range(NT) for g in range(KC // GC)]
        load_x(x_prev, 0, 0)

        def stage1a(i):
            t, g = seq[i]
            if g == 0:
                if t + 1 < NT:
                    load_x(x_prev, t + 1, (t + 1) % 2)
                else:
                    load_x(x_main, 0, (t + 1) % 2)
                if t == 0:
                    norm(0, 0)
                transposes(C_S1, C_SH1)
            if g == 2:
                norm((t + 1) % 2, 0)
            issue_casts(2)
            lru_part1a(g, i % 2)

        def stage1b(i):
            t, g = seq[i]
            lru_part1b(g, i % 2, i % 2)

        def stage2(i):
            t, g = seq[i]
            lru_part2(g, "half" if t == 0 else None, i % 2, pool_off=False)
            if t == NT - 1:
                p_group(g)
            ckpt(f"pre{t}.g{g}")

        NS_ = len(seq)
        stage1a(0)
        stage1a(1)
        stage1b(0)
        for i in range(NS_):
            if i + 2 < NS_:
                stage1a(i + 2)
            if i + 1 < NS_:
                stage1b(i + 1)
            stage2(i)
        nx = NT
        issue_casts(1000)
        op("dve", "tensor_scalar", [b_hs, b_flg], [b_hs], out=hstate[:], in0=hstate[:], scalar1=flg[:, 0:1], scalar2=None, op0=ALU.mult)
        op("dve", "tensor_scalar", [b_hrx, b_flg], [b_hrx], out=halo_rx[:], in0=halo_rx[:], scalar1=flg[:, 0:1], scalar2=None, op0=ALU.mult)
        op("dve", "tensor_scalar", [b_hp, b_flg], [b_hp], out=halo_p[:], in0=halo_p[:], scalar1=flg[:, 0:1], scalar2=None, op0=ALU.mult)

        transposes(C_S1, C_SH1)
        for t in range(NT):
            xb = nx % 2
            ckpt(f"main{t}.tr")
            for g in range(KC // GC):
                mixer_group(g, t == 0)
                ckpt(f"main{t}.g{g}")
            if t + 1 < NT:
                load_x(x_main, t + 1, (nx + 1) % 2)
            out_proj(xb)
            ckpt(f"main{t}.outproj")
            transposes(C_S2, C_SH2)
            ckpt(f"main{t}.tr2")
            ffn_up()
            ckpt(f"main{t}.ffnup")
            if t + 1 < NT:
                norm((nx + 1) % 2, 0)
            ffn_down(xb)
            ckpt(f"main{t}.ffndown")
            if t + 1 < NT:
                transposes(C_S1, C_SH1)
            final_norm_store(xb, t)
            ckpt(f"main{t}.store")
            nx += 1
        for sc_ in s_o:
            sem_, val_ = sc_
            if val_:
                S.E["pool"]["prog"].append(lambda sem_=sem_, val_=val_: nc.gpsimd.wait_ge(sem_, val_))

        with nc.Block() as block:
            @block.tensor
            def _(e):
                for f in S.E["pe"]["prog"]:
                    f()

            @block.scalar
            def _(e):
                for f in S.E["act"]["prog"]:
                    f()

            @block.vector
            def _(e):
                for f in S.E["dve"]["prog"]:
                    f()

            @block.gpsimd
            def _(e):
                for f in S.E["pool"]["prog"]:
                    f()

            @block.sync
            def _(e):
                for f in S.E["sp"]["prog"]:
                    f()
    return nc


def kernel(x, c, w_ada, b_ada, g_norm_mix, w_in, conv_a_w, conv_b_w, conv_b_bias,
           w_rg_a, b_rg_a, w_rg_x, b_rg_x, lru_lambda, w_out, g_norm_ffn,
           w_gate_up, w_down, g_norm_final, _return_maps=False):
    f32 = np.float32
    x = np.asarray(x, f32)
    ca = lambda a: np.ascontiguousarray(np.asarray(a, f32))
    shared = dict(
        w_ada=ca(w_ada[0]), w_in=ca(w_in[0]), w_rg_a=ca(w_rg_a[0]), w_rg_x=ca(w_rg_x[0]),
        w_out=ca(w_out[0]), w_gu=ca(w_gate_up[0]), w_down=ca(w_down[0]),
    )
    in_maps = []
    for i in range(NCORES):
        b, h = divmod(i, 2)
        rows = [np.asarray(c, f32)[b]] + [np.asarray(b_ada, f32)[0].reshape(6, D)[m] for m in range(6)]
        rows += [np.asarray(g_norm_mix, f32)[0]]
        rows += [np.asarray(conv_a_w, f32)[0][k] for k in range(3)]
        rows += [np.asarray(conv_b_w, f32)[0][k] for k in range(4)]
        rows += [np.asarray(conv_b_bias, f32)[0], np.asarray(b_rg_a, f32)[0], np.asarray(b_rg_x, f32)[0],
                 np.asarray(lru_lambda, f32)[0], np.asarray(g_norm_ffn, f32)[0], np.asarray(g_norm_final, f32)]
        vecs = np.ascontiguousarray(np.stack(rows, 0).reshape(NV * 8, 128))
        flags = np.zeros((128, 2), f32)
        flags[:, 0] = 1.0 if h == 1 else 0.0
        flags[:, 1] = 0.5 if h == 0 else 0.0
        xm = np.ascontiguousarray(x[b, h * TOK:(h + 1) * TOK])
        xp = np.ascontiguousarray(x[b, 0:TOK])
        m = dict(shared)
        m.update(x_main=xm, x_prev=xp, vecs=vecs, flags=flags)
        in_maps.append(m)
    if _return_maps:
        return in_maps
    nc = build_nc()
    res = run_bass_kernel_spmd(nc, in_maps, core_ids=list(range(NCORES)))
    outp = np.empty((4, 2 * TOK, D), f32)
    for i in range(NCORES):
        b, h = divmod(i, 2)
        outp[b, h * TOK:(h + 1) * TOK] = res.results[i]["out"]
    return outp
```
